# Optimizing a Trainium2 kernel written in Bass

```python
import math
import jax, jax.numpy as jnp
from jax import lax
import numpy as np

D_MODEL = 1024
BATCH = 32
SEQ = 256
DEPTH = 2
DEC_BATCH = 4
DEC_SEQ = 1024
PAST_LEN = 512

GRID_W = 64
Q_BLOCK = 128
HEAD_DIM = 64
N_HEADS_A = D_MODEL // 2 // HEAD_DIM
N_KV_A = N_HEADS_A // 4
N_HEADS_B = D_MODEL // 4 // HEAD_DIM
DIFF_DIM = HEAD_DIM // 2
GROUP_C = 64
N_GROUPS_C = D_MODEL // 4 // GROUP_C
WIDTH_A = N_HEADS_A * HEAD_DIM
WIDTH_B = N_HEADS_B * HEAD_DIM
WIDTH_C = N_GROUPS_C * GROUP_C
MIX_WIDTH = WIDTH_A + WIDTH_B + WIDTH_C
KV_WIDTH_A = N_KV_A * HEAD_DIM
PROJ_SPLITS = (WIDTH_A, KV_WIDTH_A, KV_WIDTH_A, WIDTH_A,
               WIDTH_B, WIDTH_B, WIDTH_B, WIDTH_B,
               WIDTH_C, WIDTH_C)
D_IN = sum(PROJ_SPLITS)
RMS_EPS = 1e-6
ROPE_BASE = 10000.0

kernel_name = 'hybrid_diffusion_parallel_heads_step'


def rms_norm(x, g):
    xf = x.astype(jnp.float32)
    y = xf * lax.rsqrt(jnp.mean(xf * xf, axis=-1, keepdims=True) + RMS_EPS)
    return (y * g.astype(jnp.float32)).astype(x.dtype)


def split_columns(p):
    idx, acc = [], 0
    for w in PROJ_SPLITS[:-1]:
        acc += w
        idx.append(acc)
    return jnp.split(p, idx, axis=-1)


def axial_angles(n_tok, dim):
    rows = n_tok // GRID_W
    row = jnp.repeat(jnp.arange(rows), GRID_W).astype(jnp.float32)
    col = jnp.tile(jnp.arange(GRID_W), rows).astype(jnp.float32)
    n_freq = dim // 4
    inv = 1.0 / (ROPE_BASE ** (jnp.arange(n_freq, dtype=jnp.float32) / n_freq))
    return row[:, None] * inv, col[:, None] * inv


def rope_half(x, ang):
    shp = (ang.shape[0],) + (1,) * (x.ndim - 3) + (ang.shape[1],)
    cos = jnp.cos(ang).reshape(shp).astype(x.dtype)
    sin = jnp.sin(ang).reshape(shp).astype(x.dtype)
    x1, x2 = jnp.split(x, 2, axis=-1)
    return jnp.concatenate([x1 * cos - x2 * sin, x1 * sin + x2 * cos], axis=-1)


def rope_2d(x, angs):
    xr, xc = jnp.split(x, 2, axis=-1)
    return jnp.concatenate([rope_half(xr, angs[0]), rope_half(xc, angs[1])], axis=-1)


def sweep_query_blocks(fn, *qs):
    b, s = qs[0].shape[:2]
    nb = s // Q_BLOCK
    blocks = tuple(jnp.moveaxis(q.reshape((b, nb, Q_BLOCK) + q.shape[2:]), 1, 0) for q in qs)
    out = lax.map(lambda qb: fn(*qb), blocks)
    return jnp.moveaxis(out, 0, 1).reshape((b, s) + out.shape[3:])


def gqa_attention(q, k, v):
    b, _, h, d = q.shape
    kvh = k.shape[2]
    g = h // kvh
    scale = d ** -0.5

    def block(qb):
        qb = qb.reshape(b, Q_BLOCK, kvh, g, d)
        sc = jnp.einsum('bqhgd,bkhd->bhgqk', qb, k).astype(jnp.float32) * scale
        pr = jax.nn.softmax(sc, axis=-1).astype(v.dtype)
        o = jnp.einsum('bhgqk,bkhd->bqhgd', pr, v)
        return o.reshape(b, Q_BLOCK, h, d)

    return sweep_query_blocks(block, q)


def diff_attention(q1, q2, k1, k2, v, lam):
    scale = q1.shape[-1] ** -0.5

    def block(q1b, q2b):
        s1 = jnp.einsum('bqhd,bkhd->bhqk', q1b, k1).astype(jnp.float32) * scale
        s2 = jnp.einsum('bqhd,bkhd->bhqk', q2b, k2).astype(jnp.float32) * scale
        pr = jax.nn.softmax(s1, axis=-1) - lam * jax.nn.softmax(s2, axis=-1)
        return jnp.einsum('bhqk,bkhd->bqhd', pr.astype(v.dtype), v)

    return sweep_query_blocks(block, q1, q2)


def fourier_mix(u, w_c):
    b, s, _ = u.shape
    uf = u.astype(jnp.float32).reshape(b, s, N_GROUPS_C, GROUP_C)
    f = jnp.fft.fft2(uf, axes=(1, 3), norm='ortho').real
    return f.reshape(b, s, WIDTH_C).astype(u.dtype) @ w_c


def modulation(cond, w_mod, b_mod):
    m = jax.nn.silu(cond) @ w_mod + b_mod
    return jnp.split(m, 3, axis=-1)


def mixer_layer(x, shift, scale, gate, norm_g, w_in, qn_a, kn_a, qn_b, kn_b,
                lq1, lk1, lq2, lk2, subln_g, w_c, w_out, layer_idx,
                angs_a, angs_b, ctx_a, ctx_b):
    b, s, _ = x.shape
    h = rms_norm(x, norm_g) * (1.0 + scale) + shift
    qa, ka, va, ga, qb, kb, vb, gb, uc, gc = split_columns(h @ w_in)

    qa = rms_norm(qa.reshape(b, s, N_HEADS_A, HEAD_DIM), qn_a)
    ka = rms_norm(ka.reshape(b, s, N_KV_A, HEAD_DIM), kn_a)
    va = va.reshape(b, s, N_KV_A, HEAD_DIM)
    qb = rms_norm(qb.reshape(b, s, N_HEADS_B, 2, DIFF_DIM), qn_b)
    kb = rms_norm(kb.reshape(b, s, N_HEADS_B, 2, DIFF_DIM), kn_b)
    vb = vb.reshape(b, s, N_HEADS_B, HEAD_DIM)

    if ctx_a is None:
        new_a = jnp.stack([ka, va], axis=2)
        new_b = jnp.stack([kb.reshape(b, s, N_HEADS_B, HEAD_DIM), vb], axis=2)
        ka_all, va_all, kb_all, vb_all = ka, va, kb, vb
    else:
        new_a, new_b = None, None
        qa, ka = rope_2d(qa, angs_a), rope_2d(ka, angs_a)
        qb, kb = rope_2d(qb, angs_b), rope_2d(kb, angs_b)
        n_ctx = ctx_a.shape[1]
        ka_all = jnp.concatenate([ka, ctx_a[:, :, 0]], axis=1)
        va_all = jnp.concatenate([va, ctx_a[:, :, 1]], axis=1)
        kb_ctx = ctx_b[:, :, 0].reshape(b, n_ctx, N_HEADS_B, 2, DIFF_DIM)
        kb_all = jnp.concatenate([kb, kb_ctx], axis=1)
        vb_all = jnp.concatenate([vb, ctx_b[:, :, 1]], axis=1)

    oa = gqa_attention(qa, ka_all, va_all).reshape(b, s, WIDTH_A)

    lam_init = 0.8 - 0.6 * math.exp(-0.3 * layer_idx)
    lam = (jnp.exp(jnp.sum(lq1.astype(jnp.float32) * lk1.astype(jnp.float32)))
           - jnp.exp(jnp.sum(lq2.astype(jnp.float32) * lk2.astype(jnp.float32))) + lam_init)
    ob = diff_attention(qb[..., 0, :], qb[..., 1, :], kb_all[..., 0, :], kb_all[..., 1, :], vb_all, lam)
    ob = (rms_norm(ob, subln_g) * (1.0 - lam_init)).reshape(b, s, WIDTH_B)

    oc = fourier_mix(uc, w_c)

    mix = jnp.concatenate([oa * jax.nn.silu(ga), ob * jax.nn.silu(gb), oc * jax.nn.silu(gc)], axis=-1)
    return x + gate * (mix @ w_out), new_a, new_b


def setup_inputs(seed: int = 0) -> dict:
    key = jax.random.key(seed)
    ks = jax.random.split(key, 24)
    f32 = jnp.float32
    nrm = lambda k, shp: jax.random.normal(k, shp, f32)
    return {
        'x_prompt': nrm(ks[0], (BATCH, SEQ, D_MODEL)),
        'x_sample': nrm(ks[1], (DEC_BATCH, DEC_SEQ, D_MODEL)),
        'cache_attn_a': nrm(ks[2], (DEC_BATCH, DEPTH, PAST_LEN, 2, N_KV_A, HEAD_DIM)),
        'cache_attn_b': nrm(ks[3], (DEC_BATCH, DEPTH, PAST_LEN, 2, N_HEADS_B, HEAD_DIM)),
        'c': nrm(ks[4], (DEC_BATCH, D_MODEL)),
        'c_ctx': nrm(ks[5], (D_MODEL,)),
        'norm_g': 1.0 + 0.01 * nrm(ks[6], (DEPTH, D_MODEL)),
        'w_mod': nrm(ks[7], (DEPTH, D_MODEL, 3 * D_MODEL)) * (0.5 * D_MODEL ** -0.5),
        'b_mod': 0.01 * nrm(ks[8], (DEPTH, 3 * D_MODEL)),
        'w_in': nrm(ks[9], (DEPTH, D_MODEL, D_IN)) * D_MODEL ** -0.5,
        'q_norm_a': 1.0 + 0.01 * nrm(ks[10], (DEPTH, HEAD_DIM)),
        'k_norm_a': 1.0 + 0.01 * nrm(ks[11], (DEPTH, HEAD_DIM)),
        'q_norm_b': 1.0 + 0.01 * nrm(ks[12], (DEPTH, DIFF_DIM)),
        'k_norm_b': 1.0 + 0.01 * nrm(ks[13], (DEPTH, DIFF_DIM)),
        'lambda_q1': 0.1 * nrm(ks[14], (DEPTH, DIFF_DIM)),
        'lambda_k1': 0.1 * nrm(ks[15], (DEPTH, DIFF_DIM)),
        'lambda_q2': 0.1 * nrm(ks[16], (DEPTH, DIFF_DIM)),
        'lambda_k2': 0.1 * nrm(ks[17], (DEPTH, DIFF_DIM)),
        'subln_g': 1.0 + 0.01 * nrm(ks[18], (DEPTH, HEAD_DIM)),
        'w_fourier': nrm(ks[19], (DEPTH, WIDTH_C, WIDTH_C)) * WIDTH_C ** -0.5,
        'w_out': nrm(ks[20], (DEPTH, MIX_WIDTH, D_MODEL)) * MIX_WIDTH ** -0.5,
    }


def reference(x_prompt, x_sample, cache_attn_a, cache_attn_b, c, c_ctx,
              norm_g, w_mod, b_mod, w_in, q_norm_a, k_norm_a, q_norm_b, k_norm_b,
              lambda_q1, lambda_k1, lambda_q2, lambda_k2, subln_g, w_fourier, w_out):
    y_prompt = x_prompt
    kv_a_layers, kv_b_layers = [], []
    for l in range(DEPTH):
        shift, scale, gate = modulation(c_ctx, w_mod[l], b_mod[l])
        y_prompt, kv_a, kv_b = mixer_layer(
            y_prompt, shift, scale, gate, norm_g[l], w_in[l],
            q_norm_a[l], k_norm_a[l], q_norm_b[l], k_norm_b[l],
            lambda_q1[l], lambda_k1[l], lambda_q2[l], lambda_k2[l],
            subln_g[l], w_fourier[l], w_out[l], l, None, None, None, None)
        kv_a_layers.append(kv_a)
        kv_b_layers.append(kv_b)
    new_attn_a = jnp.stack(kv_a_layers, axis=1)
    new_attn_b = jnp.stack(kv_b_layers, axis=1)

    n_lat = x_sample.shape[1]
    angs_a = axial_angles(n_lat, HEAD_DIM)
    angs_b = axial_angles(n_lat, DIFF_DIM)
    y_sample = x_sample
    for l in range(DEPTH):
        shift, scale, gate = modulation(c, w_mod[l], b_mod[l])
        y_sample, _, _ = mixer_layer(
            y_sample, shift[:, None, :], scale[:, None, :], gate[:, None, :],
            norm_g[l], w_in[l], q_norm_a[l], k_norm_a[l], q_norm_b[l], k_norm_b[l],
            lambda_q1[l], lambda_k1[l], lambda_q2[l], lambda_k2[l],
            subln_g[l], w_fourier[l], w_out[l], l, angs_a, angs_b,
            cache_attn_a[:, l], cache_attn_b[:, l])
    return (y_prompt, y_sample, new_attn_a, new_attn_b)
```

```python
import math
import os
from contextlib import ExitStack
import numpy as np
import ml_dtypes
import concourse.bass as bass
import concourse.mybir as mybir
from concourse.bass_utils import run_bass_kernel_spmd

F32 = mybir.dt.float32
BF16 = mybir.dt.bfloat16
ALU = mybir.AluOpType
POWOP = ALU.pow
AF = mybir.ActivationFunctionType
AX = mybir.AxisListType
EPS = 1e-6


class Res:
    __slots__ = ("name", "w", "r", "excl")

    def __init__(self, name):
        self.name = name
        self.w = []
        self.r = {}
        self.excl = name.startswith("bank")


class Eng:
    def __init__(self, name, is_pe=False):
        self.name = name
        self.is_pe = is_pe
        self.tick = 0
        self.obs = {}
        self.prog = []
        self.pending = False


class Prog:
    ENGS = ("pe", "act", "dve", "pool", "sp")

    def __init__(self, n_dma_sems=10):
        self.e = {n: Eng(n, n == "pe") for n in self.ENGS}
        self.n_dma_sems = n_dma_sems
        self.dma_cnt = {}
        self.dma_rr = {n: 0 for n in self.ENGS}
        self.n_inst = 0
        self.disabled = False
        self.marks = 0
        self.limit = int(os.environ.get("KSTOP", "0"))
        self.oplimit = int(os.environ.get("KOPS", "0"))

    def mark(self, name=""):
        self.marks += 1
        if self.limit and self.marks >= self.limit and not self.disabled:
            self.disabled = True
            print("KSTOP at mark", self.marks, name)

    def _deps(self, eng, reads, writes):
        deps = {}
        own = "t_" + eng.name
        for r in reads:
            for (k, v) in r.w:
                if deps.get(k, 0) < v:
                    deps[k] = v
            if r.excl:
                for k, v in r.r.items():
                    if k != own and deps.get(k, 0) < v:
                        deps[k] = v
        for w in writes:
            for (k, v) in w.w:
                if deps.get(k, 0) < v:
                    deps[k] = v
            for k, v in w.r.items():
                if deps.get(k, 0) < v:
                    deps[k] = v
        for k, v in deps.items():
            if eng.is_pe and k == "t_pe":
                continue
            if eng.obs.get(k, 0) < v:
                eng.prog.append(("wait", k, v))
                eng.obs[k] = v

    def op(self, engname, fn, reads=(), writes=(), sig=True):
        if self.oplimit:
            sig = True
            if self.n_inst >= self.oplimit:
                self.disabled = True
        if self.disabled:
            return
        eng = self.e[engname]
        self._deps(eng, reads, writes)
        key = "t_" + engname
        ev = (key, eng.tick + 1)
        if sig:
            eng.tick += 1
            eng.pending = False
        else:
            eng.pending = True
        eng.prog.append(("op", fn, key if sig else None))
        for r in reads:
            if r.r.get(key, 0) < ev[1]:
                r.r[key] = ev[1]
        for w in writes:
            w.w = [ev]
            w.r = {}
        self.n_inst += 1

    def dma(self, engname, fn, reads=(), writes=()):
        if self.oplimit and self.n_inst >= self.oplimit:
            self.disabled = True
        if self.disabled:
            return
        eng = self.e[engname]
        self._deps(eng, reads, writes)
        i = self.dma_rr[engname]
        self.dma_rr[engname] = (i + 1) % self.n_dma_sems
        key = "d_%s_%d" % (engname, i)
        prev = self.dma_cnt.get(key, 0)
        if prev and eng.obs.get(key, 0) < prev:
            eng.prog.append(("wait", key, prev))
            eng.obs[key] = prev
        val = prev + 16
        self.dma_cnt[key] = val
        eng.prog.append(("dma", fn, key))
        for r in reads:
            if r.r.get(key, 0) < val:
                r.r[key] = val
        for w in writes:
            w.w = [(key, val)]
            w.r = {}
        self.n_inst += 1

    def finish(self, out_res):
        eng = self.e["sp"]
        self._deps(eng, out_res, ())
        for n, e in self.e.items():
            assert not e.pending, "engine %s has unsignalled trailing ops" % n

    def build(self, nc, stack):
        sems = {}
        keys = ["t_" + n for n in self.ENGS] + sorted(self.dma_cnt.keys())
        for k in keys:
            sems[k] = stack.enter_context(nc.semaphore(k))
        block = stack.enter_context(nc.Block())

        def replay(name):
            def body(h):
                for item in self.e[name].prog:
                    if item[0] == "wait":
                        h.wait_ge(sems[item[1]], item[2])
                    elif item[0] == "op":
                        ins = item[1](h)
                        if item[2] is not None:
                            ins.then_inc(sems[item[2]], 1)
                    else:
                        ins = item[1](h)
                        ins.then_inc(sems[item[2]], 16)
            return body

        block.tensor(replay("pe"))
        block.scalar(replay("act"))
        block.vector(replay("dve"))
        block.gpsimd(replay("pool"))
        block.sync(replay("sp"))


LAM_INIT = [0.8 - 0.6 * math.exp(-0.3 * l) for l in range(2)]


def build_program():
    nc = bass.Bass("TRN2", target_bir_lowering=False)
    di = lambda name, shape, dt=F32: nc.dram_tensor(name, shape, dt, kind="ExternalInput").ap()
    do = lambda name, shape: nc.dram_tensor(name, shape, F32, kind="ExternalOutput").ap()
    d_xp = di("xp", [1024, 1024])
    d_xs = di("xs", [1024, 1024])
    d_ca = di("ca", [2, 512, 256])
    d_cb = di("cb", [2, 512, 512])
    d_condT = di("condT", [128, 16])
    d_bmodT = di("bmodT", [128, 48])
    d_normgT = di("normgT", [128, 16])
    d_smallv = di("smallv", [2, 384])
    d_wmod = di("w_mod", [2, 1024, 3072])
    d_win = di("w_in", [2, 1024, 2816])
    d_wout = di("w_out", [2, 1024, 1024])
    d_wc = di("w_fourier", [2, 256, 256])
    d_identf = di("identf", [128, 128])
    d_identb = di("identb", [128, 128], BF16)
    d_bcs = di("bcs", [128, 256], BF16)
    d_dftp = di("dftp", [128, 1024], BF16)
    d_dfts = di("dfts", [8, 128, 2048], BF16)
    d_rope = di("rope", [8, 128, 192])
    d_masks = di("masks", [128, 2])
    d_wsc = nc.dram_tensor("wsc", [1024, 2816], BF16, kind="Internal").ap()
    o_yp = do("yp", [1024, 1024])
    o_ys = do("ys", [512, 1024])
    o_na = do("na", [4, 2, 256, 256])
    o_nb = do("nb", [4, 2, 256, 512])

    st = ExitStack()
    with st:
        sb = lambda name, shape, dt: st.enter_context(nc.sbuf_tensor("sb_" + name, shape, dt))
        P = Prog()
        _res = {}

        def R(name):
            if name not in _res:
                _res[name] = Res(name)
            return _res[name]

        xs = sb("xs", [128, 8, 1024], F32)
        arena = sb("arena", [128, 8192], F32)
        xp = arena[:, :].rearrange("p (t n) -> p t n", t=8)
        abf = arena[:, :].bitcast(BF16)
        sQTA = abf[:, 0:2048].rearrange("p (j t) -> p j t", j=4)
        sQTB = abf[:, 2048:4096].rearrange("p (a c t) -> p a c t", a=2, c=2)
        sKTA = abf[:, 4096:5632]
        sKTB = abf[:, 5632:8704].rearrange("p (a t) -> p a t", a=2)
        sVA = abf[:, 8704:10264].rearrange("p (c g e) -> p c g e", c=12, g=2)
        sVB = abf[:, 10264:13384].rearrange("p (c g e) -> p c g e", c=12, g=4)
        sCK = abf[:, 13384:14920].rearrange("p (c n) -> p c n", c=4)
        win = sb("win", [128, 8, 2816], BF16)
        wout = sb("wout", [128, 8, 1024], BF16)
        wc = sb("wc", [128, 2, 2, 256], BF16)
        ring = sb("ring", [128, 4, 1024], BF16)
        SG = sb("SG", [128, 4, 1024], BF16)
        ZW = sb("ZW", [128, 8, 512], BF16)
        zwf = ZW[:, :, :].rearrange("p a b -> p (a b)")
        pQTA = zwf[:, 1024:2048].rearrange("p (j t) -> p j t", j=4)
        pQTB = zwf[:, 2048:3072].rearrange("p (a c t) -> p a c t", a=2, c=2)
        pKTA = zwf[:, 3072:3328]
        pKTB = zwf[:, 3584:4096].rearrange("p (a t) -> p a t", a=2)
        pVA = sb("pVA", [128, 2, 2, 65], BF16)
        pVB = sb("pVB", [128, 2, 4, 65], BF16)
        UT = sb("UT", [128, 256], BF16)
        FT = sb("FT", [128, 2, 512], BF16)
        identf = sb("identf_s", [128, 128], F32)
        identb = sb("identb_s", [128, 128], BF16)
        onesf = sb("onesf", [128, 128], F32)
        bcs = sb("bcs_s", [128, 256], BF16)
        dftp = sb("dftp_s", [128, 2, 2, 256], BF16)
        rope = sb("rope_s", [128, 4, 192], F32)
        smallv = sb("smallv_s", [128, 2, 384], F32)
        masks = sb("masks_s", [128, 2], F32)
        cm05 = sb("cm05", [128, 32], F32)
        condT = sb("condT_s", [128, 8, 2], F32)
        scT = sb("scT", [128, 8, 2], F32)
        scTb = sb("scTb", [128, 8, 2], BF16)
        bmodT = sb("bmodT_s", [128, 2, 24], F32)
        normgT = sb("normgT_s", [128, 2, 8], F32)
        modT = sb("modT", [128, 2, 24, 2], F32)
        gsT = sb("gsT", [128, 2, 8, 2], F32)
        lamn = sb("lamn", [128, 2], F32)
        lamt = sb("lamt", [128, 8], F32)
        ssx = sb("ssx", [128, 24], F32)
        rstdx = sb("rstdx", [128, 24], F32)
        ssq = sb("ssq", [128, 32], F32)
        rq = sb("rq", [128, 32], F32)
        recA = sb("recA", [128, 8], F32)
        recB = sb("recB", [128, 8], F32)
        ssb = sb("ssb", [128, 8], F32)
        gb = sb("gb", [128, 1024], F32)
        dg = sb("dg", [128, 2, 128], F32)
        thb = sb("thb", [128, 1024], BF16)
        hT2 = sb("hT", [128, 2, 8, 128], BF16)
        S1 = sb("S1", [128, 1152], F32)
        S2 = sb("S2", [128, 1152], F32)
        S3 = sb("S3", [128, 1152], F32)
        qkb = sb("qkb", [128, 1152], BF16)
        S4 = sb("S4", [128, 1152], F32)
        ostage = S4[:, 0:768]
        PT = sb("PT", [128, 2, 1024], BF16)
        mix = sb("mix", [128, 1024], BF16)
        mixT = sb("mixT", [128, 8, 128], BF16)
        obt = sb("obt", [128, 256], F32)[:, :]
        sqb = sb("sqb", [128, 256], F32)[:, :]
        PS = st.enter_context(nc.psum_tensor("PS", [128, 4096], F32))
        RB = [R("bank%d" % b) for b in range(8)]

        def bank(b, c0=0, n=512):
            return PS[:, b * 512 + c0:b * 512 + c0 + n]

        def bank_bf(b, c0=0, n=1024):
            return PS[:, b * 512:(b + 1) * 512].bitcast(BF16)[:, c0:c0 + n]

        ARENA_S = [R(n) for n in ("sQTA", "sQTB", "sKTA", "sKTB", "sVA", "sVB", "sCK")]
        XP = [R("xp%d" % t) for t in range(8)]
        XS = [R("xs%d" % t) for t in range(8)]
        ZWR = [R("zw%d" % t) for t in range(8)]
        SGR = [R("sg%d" % t) for t in range(4)]
        HT2 = [[R("hT%d_%d" % (p_, c)) for c in range(8)] for p_ in range(2)]
        WIN = [[R("win%d_%d" % (k, h)) for h in range(2)] for k in range(8)]
        RING = [R("ring%d" % i) for i in range(4)]
        PTR = [R("pt0"), R("pt1")]
        WOUT = [R("wout%d" % k) for k in range(8)]
        OST = [R("ost_KA"), R("ost_VA"), R("ost_KB"), R("ost_VB")]

        def ld(eng, dst, src, w, r=()):
            P.dma(eng, lambda e: e.dma_start(out=dst, in_=src), reads=list(r), writes=list(w))

        ld("sp", identf[:], d_identf[:, :], [R("identf")])
        ld("sp", identb[:], d_identb[:, :], [R("identb")])
        ld("sp", condT[:].rearrange("p a b -> p (a b)"), d_condT[:, :], [R("condT")])
        ld("sp", bmodT[:].rearrange("p a b -> p (a b)"), d_bmodT[:, :], [R("bmodT")])
        ld("sp", normgT[:].rearrange("p a b -> p (a b)"), d_normgT[:, :], [R("normgT")])
        ld("sp", smallv[:], d_smallv.partition_broadcast(128), [R("smallv")])
        ld("sp", masks[:], d_masks[:, :], [R("masks")])
        ld("sp", bcs[:], d_bcs[:, :], [R("bcs")])
        ld("sp", dftp[:].rearrange("p a b c -> p (a b c)"), d_dftp[:, :], [R("dftp")])
        for t in range(8):
            ld("act", xs[:, t, :], d_xs[t * 128:(t + 1) * 128, :], [XS[t]])
        P.op("pool", lambda e: e.memset(onesf[:], 1.0), writes=[R("onesf")])
        P.op("pool", lambda e: e.memset(cm05[:], -0.5), writes=[R("cm05")])
        P.op("dve", lambda e: e.memset(ssq[:], 1.0), writes=[R("ssq_QA"), R("ssq_KA"), R("ssq_KB"), R("ssq_QB")])
        for l in range(2):
            P.dma("pool", lambda e, l=l: e.dma_start(out=wc[:, l, :, :], in_=d_wc[l].rearrange("(c p) n -> p c n", p=128)),
                  writes=[R("wc")])

        P.op("act", lambda e: e.activation(out=scT[:], in_=condT[:], func=AF.Tanh, scale=0.5), reads=[R("condT")], writes=[R("scT")])
        P.op("dve", lambda e: e.scalar_tensor_tensor(out=scT[:], in0=scT[:], scalar=1.0, in1=condT[:], op0=ALU.add, op1=ALU.mult),
             reads=[R("scT"), R("condT")], writes=[R("scT")])
        P.op("dve", lambda e: e.tensor_scalar(out=scT[:], in0=scT[:], scalar1=0.5, scalar2=None, op0=ALU.mult), reads=[R("scT")], writes=[R("scT")])
        P.op("dve", lambda e: e.tensor_copy(out=scTb[:], in_=scT[:]), reads=[R("scT")], writes=[R("scTb")])
        P.op("dve", lambda e: e.tensor_scalar(out=smallv[:, :, 0:64], in0=smallv[:, :, 0:64], scalar1=64 ** -0.5, scalar2=None, op0=ALU.mult),
             reads=[R("smallv")], writes=[R("smallv")])
        P.op("dve", lambda e: e.tensor_scalar(out=smallv[:, :, 160:192], in0=smallv[:, :, 160:192], scalar1=32 ** -0.5, scalar2=None, op0=ALU.mult),
             reads=[R("smallv")], writes=[R("smallv")])
        for l in range(2):
            P.op("dve", lambda e, l=l: e.tensor_scalar(out=smallv[:, l, 192:256], in0=smallv[:, l, 192:256], scalar1=1.0 - LAM_INIT[l], scalar2=None, op0=ALU.mult),
                 reads=[R("smallv")], writes=[R("smallv")])
        for l in range(2):
            P.op("dve", lambda e, l=l: e.tensor_tensor(out=S1[:, 0:32], in0=smallv[:, l, 256:288], in1=smallv[:, l, 288:320], op=ALU.mult),
                 reads=[R("smallv")], writes=[R("S1_QA")])
            P.op("dve", lambda e, l=l: e.tensor_tensor(out=S1[:, 32:64], in0=smallv[:, l, 320:352], in1=smallv[:, l, 352:384], op=ALU.mult),
                 reads=[R("smallv")], writes=[R("S1_QA")])
            P.op("dve", lambda e, l=l: e.tensor_reduce(out=lamt[:, 0:2], in_=S1[:, 0:64].rearrange("p (a b) -> p a b", a=2), axis=AX.X, op=ALU.add),
                 reads=[R("S1_QA")], writes=[R("lamt")])
            P.op("act", lambda e, l=l: e.activation(out=lamt[:, 2:4], in_=lamt[:, 0:2], func=AF.Exp), reads=[R("lamt")], writes=[R("lamt")])
            P.op("dve", lambda e, l=l: e.tensor_tensor(out=lamt[:, 4:5], in0=lamt[:, 3:4], in1=lamt[:, 2:3], op=ALU.subtract),
                 reads=[R("lamt")], writes=[R("lamt")])
            P.op("dve", lambda e, l=l: e.tensor_scalar(out=lamn[:, l:l + 1], in0=lamt[:, 4:5], scalar1=-LAM_INIT[l], scalar2=None, op0=ALU.add),
                 reads=[R("lamt")], writes=[R("lamn")])

        def load_win(l, parts=None):
            for k in range(8):
                for hlf in range(2):
                    if parts is not None and (k * 2 + hlf) not in parts:
                        continue
                    P.dma("pool", lambda e, l=l, k=k, hlf=hlf: e.dma_start(
                        out=win[:, k, hlf * 1408:(hlf + 1) * 1408], in_=d_win[l][k * 128:(k + 1) * 128, hlf * 1408:(hlf + 1) * 1408]),
                        writes=[WIN[k][hlf]])

        WSC = [[R("wsc%d_%d" % (k, h)) for h in range(2)] for k in range(8)]

        def precast_win1():
            for k in range(8):
                for hlf in range(2):
                    P.dma("pool", lambda e, k=k, hlf=hlf: e.dma_start(
                        out=d_wsc[k * 128:(k + 1) * 128, hlf * 1408:(hlf + 1) * 1408], in_=d_win[1][k * 128:(k + 1) * 128, hlf * 1408:(hlf + 1) * 1408]),
                        writes=[WSC[k][hlf]])

        def load_win_fast():
            for k in range(8):
                P.dma("sp", lambda e, k=k: e.dma_start(out=win[:, k, :], in_=d_wsc[k * 128:(k + 1) * 128, :]), reads=WSC[k], writes=WIN[k])

        def load_wout(l):
            for k in range(8):
                P.dma("pool", lambda e, l=l, k=k: e.dma_start(out=wout[:, k, :], in_=d_wout[l][k * 128:(k + 1) * 128, :]), writes=[WOUT[k]])

        mod_ctr = [0]
        mod_fifo = []

        def mod_dma(l, blk):
            slot = mod_ctr[0] % 4
            mod_ctr[0] += 1
            wmv = ring[:, slot, :].rearrange("p (k n) -> p k n", k=8)
            P.dma("pool", lambda e: e.dma_start(out=wmv, in_=d_wmod[l][:, blk * 128:(blk + 1) * 128].rearrange("(k p) n -> p k n", p=128)), writes=[RING[slot]])
            mod_fifo.append((l, blk, slot, wmv))

        def mod_mm():
            l, blk, slot, wmv = mod_fifo.pop(0)
            mp = bank(7, 508, 2)
            for k in range(8):
                P.op("pe", lambda e, k=k: e.matmul(out=mp, lhsT=wmv[:, k, :], rhs=scTb[:, k, :], start=(k == 0), stop=(k == 7)),
                     reads=[RING[slot], R("scTb")], writes=[RB[7]], sig=(k == 7))
            P.op("dve", lambda e: e.tensor_scalar(out=modT[:, l, blk, :], in0=mp, scalar1=bmodT[:, l, blk:blk + 1], scalar2=None, op0=ALU.add),
                 reads=[RB[7], R("bmodT")], writes=[R("modT%d" % l)])
            if 8 <= blk < 16:
                c = blk - 8
                P.op("dve", lambda e: e.tensor_scalar(out=gsT[:, l, c, :], in0=modT[:, l, blk, :], scalar1=1.0, scalar2=normgT[:, l, c:c + 1],
                                                      op0=ALU.add, op1=ALU.mult),
                     reads=[R("modT%d" % l), R("normgT")], writes=[R("gsT%d" % l)])

        def mod_super(l, sb_):
            region = bank(7, 496, 16).rearrange("p (c j) -> p c j", j=2)
            slots = []

            def dma(k):
                slot = mod_ctr[0] % 4
                mod_ctr[0] += 1
                P.dma("pool", lambda e: e.dma_start(out=ring[:, slot, :], in_=d_wmod[l][k * 128:(k + 1) * 128, sb_ * 1024:(sb_ + 1) * 1024]), writes=[RING[slot]])
                slots.append(slot)
            for k in range(4):
                dma(k)
            for k in range(8):
                slot = slots[k]
                for cb in range(8):
                    P.op("pe", lambda e, k=k, cb=cb, slot=slot: e.matmul(out=region[:, cb, :], lhsT=ring[:, slot, cb * 128:(cb + 1) * 128], rhs=scTb[:, k, :],
                                                                       start=(k == 0 and cb == 0), stop=(k == 7), skip_group_check=True),
                         reads=[RING[slot], R("scTb")], writes=[RB[7]], sig=(cb == 7))
                if k + 4 < 8:
                    dma(k + 4)
            P.op("dve", lambda e: e.tensor_tensor(out=modT[:, l, sb_ * 8:(sb_ + 1) * 8, :], in0=region,
                                                  in1=bmodT[:, l, sb_ * 8:(sb_ + 1) * 8].unsqueeze(2).to_broadcast([128, 8, 2]), op=ALU.add),
                 reads=[RB[7], R("bmodT")], writes=[R("modT%d" % l)])
            if sb_ == 1:
                P.op("dve", lambda e: e.scalar_tensor_tensor(out=gsT[:, l, :, :], in0=modT[:, l, 8:16, :], scalar=1.0,
                                                             in1=normgT[:, l, :].unsqueeze(2).to_broadcast([128, 8, 2]), op0=ALU.add, op1=ALU.mult),
                     reads=[R("modT%d" % l), R("normgT")], writes=[R("gsT%d" % l)])

        def mod_stream(blocks, depth=3):
            for (l, blk) in blocks:
                mod_dma(l, blk)
                while len(mod_fifo) > depth:
                    mod_mm()

        def mod_flush():
            while mod_fifo:
                mod_mm()

        def build_gate(l, cond):
            for c in range(8):
                P.op("dve", lambda e, c=c: e.tensor_scalar(out=dg[:, c % 2, :], in0=identf[:], scalar1=modT[:, l, 16 + c, cond:cond + 1], scalar2=0.5,
                                                           op0=ALU.mult, op1=ALU.mult),
                     reads=[R("identf"), R("modT%d" % l)], writes=[R("dg%d" % (c % 2))])
                P.op("pe", lambda e, c=c: e.matmul(out=bank(c // 4, (c % 4) * 128, 128), lhsT=onesf[:], rhs=dg[:, c % 2, :], start=True, stop=True),
                     reads=[R("onesf"), R("dg%d" % (c % 2))], writes=[RB[c // 4]])
            P.op("act", lambda e: e.copy(out=gb[:], in_=PS[:, 0:1024]), reads=[RB[0], RB[1]], writes=[R("gb")])

        def xstats(xtiles, xres, c0=0):
            n = len(xtiles)
            sr = [R("ssx%d" % c) for c in range(c0, c0 + n, 2)]
            rr = [R("rstdx%d" % c) for c in range(c0, c0 + n, 2)]
            P.op("dve", lambda e: e.memset(ssx[:, c0:c0 + n], 0.0), writes=sr)
            for i in range(n):
                P.op("act", lambda e, i=i: e.activation(out=thb[:], in_=xtiles[i], func=AF.Square, accum_out=ssx[:, c0 + i:c0 + i + 1]),
                     reads=[xres[i]] + sr, writes=[R("thb0"), R("thb512"), R("thb768")] + sr)
            P.op("dve", lambda e: e.tensor_scalar(out=ssx[:, c0:c0 + n], in0=ssx[:, c0:c0 + n], scalar1=1.0 / 1024, scalar2=EPS, op0=ALU.mult, op1=ALU.add),
                 reads=sr, writes=sr)
            P.op("pool", lambda e: e.tensor_tensor(out=rstdx[:, c0:c0 + n], in0=ssx[:, c0:c0 + n], in1=cm05[:, 0:n], op=POWOP),
                 reads=sr + [R("cm05")], writes=rr)

        def phase1(l, cond, xt, xres, si, mode, dst, ropev=None, outs=None, par=0):
            want_q = mode in ("full", "qg")
            want_k = mode in ("full", "kvu")
            segs = []
            if want_q:
                segs.append(("QA", bank(2), 0, 512, 64, 0, RB[2], 0))
            if want_k:
                segs.append(("KA", bank(3, 0, 128), 512, 128, 64, 8, RB[3], 64))
                segs.append(("KB", bank(5, 256, 256), 640, 256, 32, 10, RB[5], 128))
            if want_q:
                segs.append(("QB", bank(5, 0, 256), 896, 256, 32, 18, RB[5], 160))

            hTv = hT2[:, par, :, :]
            HTp = HT2[par]

            def fF():
                xn = S3[:, 0:1024]
                P.op("act", lambda e: e.activation(out=xn, in_=xt, func=AF.Identity, scale=rstdx[:, si:si + 1]),
                     reads=[xres, R("rstdx%d" % (si - si % 2))], writes=[R("S3")])
                tpx = PS[:, 0:1024].rearrange("p (c t) -> p c t", c=8)
                for c in range(8):
                    P.op("pe", lambda e, c=c: e.transpose(out=tpx[:, c, :], in_=xn[:, c * 128:(c + 1) * 128], identity=identf[:]),
                         reads=[R("S3"), R("identf")], writes=[RB[c // 4]], sig=(c % 4 == 3))
                for c in range(8):
                    P.op("dve", lambda e, c=c: e.tensor_scalar(out=hTv[:, c, :], in0=tpx[:, c, :], scalar1=gsT[:, l, c, cond:cond + 1],
                                                               scalar2=modT[:, l, c, cond:cond + 1], op0=ALU.mult, op1=ALU.add),
                         reads=[RB[c // 4], R("gsT%d" % l), R("modT%d" % l)], writes=[HTp[c]])


            def proj(b, c0, n, oc0=0):
                for k in range(8):
                    P.op("pe", lambda e, k=k: e.matmul(out=bank(b, oc0, n), lhsT=hTv[:, k, :], rhs=win[:, k, c0:c0 + n], start=(k == 0), stop=(k == 7)),
                         reads=[HTp[k]] + WIN[k], writes=[RB[b]], sig=(k == 7))


            def fX():
                if want_q:
                    proj(2, 0, 512)
                if want_k:
                    proj(3, 512, 256)
                if want_k:
                    for ch in range(2):
                        for k in range(8):
                            P.op("pe", lambda e, ch=ch, k=k: e.matmul(out=bank(3, 256 + ch * 128, 128), lhsT=win[:, k, 2304 + ch * 128:2304 + (ch + 1) * 128],
                                                                      rhs=hTv[:, k, :], start=(k == 0), stop=(k == 7)),
                                 reads=[HTp[k]] + WIN[k], writes=[RB[3]], sig=(k == 7))
                if mode == "full":
                    proj(5, 1280, 512)
                elif mode == "kvu":
                    proj(5, 1536, 256, 256)
                else:
                    proj(5, 1280, 256)

            def fY():
                if want_q:
                    proj(4, 768, 512)
                    proj(7, 2560, 256)
                if mode == "full":
                    proj(6, 1792, 512)
                elif mode == "kvu":
                    proj(6, 1792, 256)
                else:
                    proj(6, 2048, 256, 256)

            def fBx():
                if want_k:
                    P.op("act", lambda e: e.copy(out=dst["va"][:, :, 0:64], in_=bank(3, 128, 128).rearrange("p (g d) -> p g d", g=2)),
                         reads=[RB[3]], writes=[dst["va_r"]])
                    if outs is not None:
                        P.op("act", lambda e: e.copy(out=ostage[:, 128:256], in_=bank(3, 128, 128)), reads=[RB[3]], writes=[R("ost_VA")])
                for (nm, src, so, n, gs, sso, br, go) in segs:
                    ng = n // gs
                    P.op("act", lambda e, src=src, so=so, n=n: e.activation(out=S1[:, so:so + n], in_=src, func=AF.Square), reads=[br], writes=[R("S1_" + nm)])
                    P.op("act", lambda e, src=src, so=so, n=n: e.copy(out=S2[:, so:so + n], in_=src), reads=[br], writes=[R("S2_" + nm)])
                    P.op("dve", lambda e, so=so, n=n, gs=gs, sso=sso, ng=ng: e.tensor_reduce(
                        out=ssq[:, sso:sso + ng], in_=S1[:, so:so + n].rearrange("p (g d) -> p g d", d=gs), axis=AX.X, op=ALU.add),
                        reads=[R("S1_" + nm)], writes=[R("ssq_" + nm)])
                    P.op("dve", lambda e, gs=gs, sso=sso, ng=ng: e.tensor_scalar(out=ssq[:, sso:sso + ng], in0=ssq[:, sso:sso + ng], scalar1=1.0 / gs, scalar2=EPS,
                                                                                 op0=ALU.mult, op1=ALU.add), reads=[R("ssq_" + nm)], writes=[R("ssq_" + nm)])
                if want_k:
                    P.op("dve", lambda e: e.tensor_copy(out=UT[:], in_=bank(3, 256, 256)), reads=[RB[3]], writes=[R("UT")])
                P.op("pool", lambda e: e.tensor_tensor(out=rq[:, 0:26], in0=ssq[:, 0:26], in1=cm05[:, 0:26], op=POWOP),
                     reads=[R("ssq_" + sg_[0]) for sg_ in segs] + [R("cm05")], writes=[R("rq")])

            def fBy():
                if want_q:
                    sg = SG[:, dst["sg"], :]
                    sgr = SGR[dst["sg"]]
                    for (b, c0, n, o) in ((4, 0, 512, 0), (6, 256, 256, 512), (7, 0, 256, 768)):
                        P.op("act", lambda e, b=b, c0=c0, n=n, o=o: e.activation(out=thb[:, o:o + n], in_=bank(b, c0, n), func=AF.Tanh, scale=0.5),
                             reads=[RB[b]], writes=[R("thb%d" % o)])
                        P.op("dve", lambda e, b=b, c0=c0, n=n, o=o: e.scalar_tensor_tensor(out=sg[:, o:o + n], in0=thb[:, o:o + n], scalar=1.0, in1=bank(b, c0, n),
                                                                                           op0=ALU.add, op1=ALU.mult),
                             reads=[R("thb%d" % o), RB[b]], writes=[R("sg%d_%d" % (dst["sg"], o))])
                if want_k:
                    P.op("act", lambda e: e.copy(out=dst["vb"][:, :, 0:64], in_=bank(6, 0, 256).rearrange("p (g d) -> p g d", g=4)),
                         reads=[RB[6]], writes=[dst["vb_r"]])
                    if outs is not None:
                        P.op("act", lambda e: e.copy(out=ostage[:, 512:768], in_=bank(6, 0, 256)), reads=[RB[6]], writes=[R("ost_VB")])

            def fB2a():
                if want_k:
                    for ch in range(2):
                        P.op("pe", lambda e, ch=ch: e.matmul(out=bank(0, ch * 256, 256), lhsT=UT[:, ch * 128:(ch + 1) * 128], rhs=bcs[:], start=True, stop=True),
                             reads=[R("UT"), R("bcs")], writes=[RB[0]])
                    P.op("act", lambda e: e.copy(out=ZW[:, dst["zw"], :], in_=bank(0)), reads=[RB[0]], writes=[ZWR[dst["zw"]]])
                for (nm, src, so, n, gs, sso, br, go) in segs:
                    ng = n // gs
                    P.op("dve", lambda e, so=so, n=n, gs=gs, sso=sso, ng=ng: e.tensor_tensor(
                        out=S2[:, so:so + n].rearrange("p (g d) -> p g d", d=gs), in0=S2[:, so:so + n].rearrange("p (g d) -> p g d", d=gs),
                        in1=rq[:, sso:sso + ng].unsqueeze(2).to_broadcast([128, ng, gs]), op=ALU.mult),
                        reads=[R("S2_" + nm), R("rq")], writes=[R("S2_" + nm)])

                def qdst_perm(nm, so, n, gs):
                    if nm == "QA":
                        return (qkb[:, 0:512].rearrange("p (j i d) -> p i j d", j=4, i=2, d=64), lambda ap: ap.rearrange("p (i j) d -> p i j d", i=2))
                    return (qkb[:, so:so + n].rearrange("p (g d) -> p g d", d=gs), lambda ap: ap)

                for (nm, src, so, n, gs, sso, br, go) in segs:
                    ng = n // gs
                    s2v = S2[:, so:so + n].rearrange("p (g d) -> p g d", d=gs)
                    gain = smallv[:, l, go:go + gs].unsqueeze(1).to_broadcast([128, ng, gs])
                    if ropev is None:
                        qdst, perm = qdst_perm(nm, so, n, gs)
                        gq = smallv[:, l, 0:64].unsqueeze(1).unsqueeze(1).to_broadcast([128, 2, 4, 64]) if nm == "QA" else gain
                        P.op("dve", lambda e, s2v=s2v, gq=gq, qdst=qdst, perm=perm: e.tensor_tensor(out=qdst, in0=perm(s2v), in1=gq, op=ALU.mult),
                             reads=[R("S2_" + nm), R("smallv")], writes=[R("qkb_" + nm)])
                        if outs is not None and nm in ("KA", "KB"):
                            oo = 0 if nm == "KA" else 256
                            P.op("dve", lambda e, s2v=s2v, gain=gain, oo=oo, n=n, gs=gs: e.tensor_tensor(
                                out=ostage[:, oo:oo + n].rearrange("p (g d) -> p g d", d=gs), in0=s2v, in1=gain, op=ALU.mult),
                                reads=[R("S2_" + nm), R("smallv")], writes=[R("ost_" + nm)])
                    else:
                        P.op("dve", lambda e, s2v=s2v, gain=gain: e.tensor_tensor(out=s2v, in0=s2v, in1=gain, op=ALU.mult),
                             reads=[R("S2_" + nm), R("smallv")], writes=[R("S2_" + nm)])
                if ropev is not None:
                    for ty, names, gs in (("A", ("QA", "KA"), 64), ("B", ("KB", "QB"), 32)):
                        sg_t = [sg_ for sg_ in segs if sg_[0] in names]
                        if not sg_t:
                            continue
                        lo = min(sg_[2] for sg_ in sg_t)
                        hi = max(sg_[2] + sg_[3] for sg_ in sg_t)
                        n = hi - lo
                        ng = n // gs
                        m = gs // 4
                        ro = 0 if gs == 64 else 128
                        cosv = ropev[:, ro:ro + gs].unsqueeze(1).to_broadcast([128, ng, gs])
                        sinv = ropev[:, ro + gs:ro + 2 * gs]
                        rd2 = [R("S2_" + sg_[0]) for sg_ in sg_t]
                        P.op("dve", lambda e, lo=lo, hi=hi, gs=gs, cosv=cosv: e.tensor_tensor(
                            out=S1[:, lo:hi].rearrange("p (g d) -> p g d", d=gs), in0=S2[:, lo:hi].rearrange("p (g d) -> p g d", d=gs), in1=cosv, op=ALU.mult),
                            reads=rd2 + [R("rope")], writes=[R("S1_" + sg_[0]) for sg_ in sg_t])
                        for xx in range(2):
                            src_ap = S2[:, lo:hi].rearrange("p (h r x m) -> p h r x m", r=2, x=2, m=m)[:, :, :, 1 - xx, :]
                            dst_ap = S4[:, lo:hi].rearrange("p (h r x m) -> p h r x m", r=2, x=2, m=m)[:, :, :, xx, :]
                            sn_ap = sinv.rearrange("p (r x m) -> p r x m", r=2, x=2, m=m)[:, :, xx, :].unsqueeze(1).to_broadcast([128, ng, 2, m])
                            P.op("pool", lambda e, src_ap=src_ap, dst_ap=dst_ap, sn_ap=sn_ap: e.tensor_tensor(out=dst_ap, in0=src_ap, in1=sn_ap, op=ALU.mult),
                                 reads=rd2 + [R("rope")], writes=[R("S4_%s%d" % (ty, xx))] + OST)
                        for (nm, src, so, n2, gs2, sso, br, go) in sg_t:
                            qdst, perm = qdst_perm(nm, so, n2, gs2)
                            s1v = S1[:, so:so + n2].rearrange("p (g d) -> p g d", d=gs2)
                            s4v = S4[:, so:so + n2].rearrange("p (g d) -> p g d", d=gs2)
                            P.op("pool", lambda e, s1v=s1v, s4v=s4v, qdst=qdst, perm=perm: e.tensor_tensor(out=qdst, in0=perm(s1v), in1=perm(s4v), op=ALU.add),
                                 reads=[R("S1_" + nm), R("S4_%s0" % ty), R("S4_%s1" % ty)] + OST, writes=[R("qkb_" + nm)])
                if outs is not None and want_k:
                    na_ap, nb_ap = outs
                    P.dma("sp", lambda e: e.dma_start(out=na_ap, in_=ostage[:, 0:256]), reads=[R("ost_KA"), R("ost_VA")], writes=[R("o_na")])
                    P.dma("sp", lambda e: e.dma_start(out=nb_ap, in_=ostage[:, 256:768]), reads=[R("ost_KB"), R("ost_VB")], writes=[R("o_nb")])

            def fB2b():
                tb = bank_bf(1)
                tb7 = bank_bf(7, 512, 512)
                if want_q:
                    for j in range(4):
                        P.op("pe", lambda e, j=j: e.transpose(out=tb[:, j * 128:(j + 1) * 128], in_=qkb[:, j * 128:(j + 1) * 128], identity=identb[:]),
                             reads=[R("qkb_QA"), R("identb")], writes=[RB[1]], sig=(j == 3))
                    P.op("dve", lambda e: e.tensor_copy(out=dst["qta"], in_=tb[:, 0:512].rearrange("p (j t) -> p j t", j=4)),
                         reads=[RB[1]], writes=[dst["qta_r"]])
                    for j in range(2):
                        P.op("pe", lambda e, j=j: e.transpose(out=tb7[:, j * 128:(j + 1) * 128], in_=qkb[:, 896 + j * 128:896 + (j + 1) * 128], identity=identb[:]),
                             reads=[R("qkb_QB"), R("identb")], writes=[RB[7]], sig=(j == 1))
                    for cc in range(2):
                        P.op("dve", lambda e, cc=cc: e.tensor_scalar(out=dst["qtb"][:, :, cc, :], in0=tb7[:, 0:256].rearrange("p (a t) -> p a t", a=2),
                                                                     scalar1=masks[:, cc:cc + 1], scalar2=None, op0=ALU.mult),
                             reads=[RB[7], R("masks")], writes=[dst["qtb_r"]])
                if want_k:
                    for j in range(3):
                        P.op("pe", lambda e, j=j: e.transpose(out=tb[:, 512 + j * 128:512 + (j + 1) * 128], in_=qkb[:, 512 + j * 128:512 + (j + 1) * 128], identity=identb[:]),
                             reads=[R("qkb_KA"), R("qkb_KB"), R("identb")], writes=[RB[1]], sig=(j == 2))
                    P.op("dve", lambda e: e.tensor_copy(out=dst["kta"], in_=tb[:, 512:640]), reads=[RB[1]], writes=[dst["kta_r"]])
                    P.op("dve", lambda e: e.tensor_copy(out=dst["ktb"], in_=tb[:, 640:896].rearrange("p (a t) -> p a t", a=2)), reads=[RB[1]], writes=[dst["ktb_r"]])

            return dict(F=fF, X=fX, Y=fY, Bx=fBx, By=fBy, B2a=fB2a, B2b=fB2b)

        att_step = [0]

        def attend(l, st_, qsl, nkc, sgslot, ft_sl, xt, xres, out_ap, out_res, alt=False):
            step = att_step

            def sbank():
                b = (step[0] % 2) * 2
                step[0] += 1
                return b

            def loop(inject=None):
                steps = []
                for kind in ("A", "B"):
                    for kc in range(nkc):
                        b0 = sbank()
                        buf = step[0] % 2
                        steps.append((kind, kc, b0, PT[:, buf, :], PTR[buf]))

                def qk(kind, kc, b0, pt, ptr):
                    ks = slice(kc * 128, (kc + 1) * 128)
                    if kind == "A":
                        for g in range(2):
                            P.op("pe", lambda e, g=g: e.matmul(out=bank(b0 + g), lhsT=st_["kta"][64 * g:64 * g + 64, ks],
                                                               rhs=st_["qta"][64 * g:64 * g + 64, :, qsl], start=True, stop=True),
                                 reads=[st_["kta_r"], st_["qta_r"]], writes=[RB[b0 + g]])
                    else:
                        for pr in range(2):
                            for i in range(2):
                                P.op("pe", lambda e, pr=pr, i=i: e.matmul(out=bank(b0 + i, pr * 256, 256), lhsT=st_["ktb"][64 * i:64 * i + 64, pr, ks],
                                                                          rhs=st_["qtb"][64 * i:64 * i + 64, pr, :, qsl], start=True, stop=True),
                                     reads=[st_["ktb_r"], st_["qtb_r"]], writes=[RB[b0 + i]], sig=True)

                def ex(kind, kc, b0, pt, ptr):
                    P.op("act", lambda e: e.activation(out=pt, in_=PS[:, b0 * 512:b0 * 512 + 1024], func=AF.Exp),
                         reads=[RB[b0], RB[b0 + 1]], writes=[ptr])

                def pv(kind, kc, b0, pt, ptr):
                    if kind == "A":
                        for g in range(2):
                            for j in range(4):
                                P.op("pe", lambda e, g=g, j=j: e.matmul(out=bank(4 + g, j * 65, 65), lhsT=pt[:, g * 512 + j * 128:g * 512 + (j + 1) * 128],
                                                                        rhs=st_["va"][:, kc, g, :], start=(kc == 0 and j == 0), stop=(kc == nkc - 1),
                                                                        skip_group_check=True),
                                     reads=[ptr, st_["va_r"]], writes=[RB[4 + g]], sig=(j == 3))
                    else:
                        for pr in range(2):
                            for i in range(2):
                                for cc in range(2):
                                    P.op("pe", lambda e, pr=pr, i=i, cc=cc: e.matmul(
                                        out=bank(6 + pr, (i * 2 + cc) * 65, 65), lhsT=pt[:, i * 512 + pr * 256 + cc * 128:i * 512 + pr * 256 + (cc + 1) * 128],
                                        rhs=st_["vb"][:, kc, 2 * pr + i, :], start=(kc == 0 and i == 0 and cc == 0), stop=(kc == nkc - 1), skip_group_check=True),
                                        reads=[ptr, st_["vb_r"]], writes=[RB[6 + pr]], sig=(i == 1 and cc == 1))

                qk(*steps[0])
                for si_, stp in enumerate(steps):
                    if si_ + 1 < len(steps):
                        qk(*steps[si_ + 1])
                    ex(*stp)
                    pv(*stp)
                    if inject and si_ in inject:
                        inject[si_]()

            sg = SG[:, sgslot, :]
            if alt:
                tA = SG[:, 2, :].bitcast(F32)
                tB = SG[:, 3, :].bitcast(F32)
                tA_r = [R("sg2_0"), R("sg2_512"), R("sg2_768")]
                tB_r = [R("sg3_0"), R("sg3_512"), R("sg3_768")]
            else:
                tA = S1[:, 0:512]
                tB = S2[:, 0:512]
                tA_r = [R("S1_QA")]
                tB_r = [R("S2_QA")]

            def post1():
                oav = PS[:, 4 * 512:6 * 512].rearrange("p (g x) -> p g x", g=2)[:, :, 0:260].rearrange("p g (j e) -> p g j e", e=65)
                P.op("dve", lambda e: e.reciprocal(out=recA[:].rearrange("p (g j) -> p g j", g=2), in_=oav[:, :, :, 64]), reads=[RB[4], RB[5]], writes=[R("recA")])
                for g in range(2):
                    P.op("dve", lambda e, g=g: e.tensor_tensor(out=tA[:, g * 256:(g + 1) * 256].rearrange("p (j d) -> p j d", j=4), in0=oav[:, g, :, 0:64],
                                                               in1=recA[:, g * 4:(g + 1) * 4].unsqueeze(2).to_broadcast([128, 4, 64]), op=ALU.mult),
                         reads=[RB[4 + g], R("recA")], writes=tA_r)
                obv = PS[:, 6 * 512:8 * 512].rearrange("p (g x) -> p g x", g=2)[:, :, 0:260].rearrange("p g (j e) -> p g j e", e=65)
                P.op("dve", lambda e: e.reciprocal(out=recB[:].rearrange("p (g j) -> p g j", g=2), in_=obv[:, :, :, 64]), reads=[RB[6], RB[7]], writes=[R("recB")])
                rbv = recB[:].rearrange("p (h c) -> p h c", c=2)
                P.op("dve", lambda e: e.tensor_scalar(out=rbv[:, :, 1], in0=rbv[:, :, 1], scalar1=lamn[:, l:l + 1], scalar2=None, op0=ALU.mult),
                     reads=[R("recB"), R("lamn")], writes=[R("recB")])
                for pr in range(2):
                    P.op("dve", lambda e, pr=pr: e.tensor_tensor(out=tB[:, pr * 256:(pr + 1) * 256].rearrange("p (j d) -> p j d", j=4), in0=obv[:, pr, :, 0:64],
                                                                 in1=recB[:, pr * 4:(pr + 1) * 4].unsqueeze(2).to_broadcast([128, 4, 64]), op=ALU.mult),
                         reads=[RB[6 + pr], R("recB")], writes=tB_r)

            def post2a(bo=2):
                P.op("dve", lambda e: e.tensor_tensor(out=mix[:, 0:512], in0=tA, in1=sg[:, 0:512], op=ALU.mult), reads=tA_r + [R("sg%d_0" % sgslot)], writes=[R("mix_A")])
                tBv = tB.rearrange("p (h c d) -> p h c d", h=4, c=2)
                obv3 = obt.rearrange("p (h d) -> p h d", h=4)
                P.op("dve", lambda e: e.tensor_tensor(out=obv3, in0=tBv[:, :, 0, :], in1=tBv[:, :, 1, :], op=ALU.add), reads=tB_r, writes=[R("OBT")])
                P.op("act", lambda e: e.activation(out=sqb, in_=obt, func=AF.Square), reads=[R("OBT")], writes=[R("SQB")])
                P.op("dve", lambda e: e.tensor_reduce(out=ssb[:, 0:4], in_=sqb.rearrange("p (h d) -> p h d", h=4), axis=AX.X, op=ALU.add),
                     reads=[R("SQB")], writes=[R("ssb")])
                P.op("dve", lambda e: e.tensor_scalar(out=ssb[:, 0:4], in0=ssb[:, 0:4], scalar1=1.0 / 64, scalar2=EPS, op0=ALU.mult, op1=ALU.add),
                     reads=[R("ssb")], writes=[R("ssb")])
                P.op("pool", lambda e: e.tensor_tensor(out=ssb[:, 4:8], in0=ssb[:, 0:4], in1=cm05[:, 0:4], op=POWOP), reads=[R("ssb"), R("cm05")], writes=[R("ssb")])
                P.op("dve", lambda e: e.tensor_tensor(out=obv3, in0=obv3, in1=ssb[:, 4:8].unsqueeze(2).to_broadcast([128, 4, 64]), op=ALU.mult),
                     reads=[R("OBT"), R("ssb")], writes=[R("OBT")])
                P.op("dve", lambda e: e.tensor_tensor(out=obv3, in0=obv3, in1=smallv[:, l, 192:256].unsqueeze(1).to_broadcast([128, 4, 64]), op=ALU.mult),
                     reads=[R("OBT"), R("smallv")], writes=[R("OBT")])
                P.op("dve", lambda e: e.tensor_tensor(out=mix[:, 512:768], in0=obt, in1=sg[:, 512:768], op=ALU.mult), reads=[R("OBT"), R("sg%d_512" % sgslot)], writes=[R("mix_B")])
                for c in range(2):
                    P.op("pe", lambda e, c=c: e.matmul(out=bank(bo, 0, 256), lhsT=FT[:, c, ft_sl], rhs=wc[:, l, c, :], start=(c == 0), stop=(c == 1)),
                         reads=[R("FT"), R("wc")], writes=[RB[bo]], sig=(c == 1))
                P.op("dve", lambda e: e.tensor_tensor(out=mix[:, 768:1024], in0=bank(bo, 0, 256), in1=sg[:, 768:1024], op=ALU.mult),
                     reads=[RB[bo], R("sg%d_768" % sgslot)], writes=[R("mix_C")])

            def post2b(bt=3):
                mtb = bank_bf(bt)
                for c in range(8):
                    P.op("pe", lambda e, c=c: e.transpose(out=mtb[:, c * 128:(c + 1) * 128], in_=mix[:, c * 128:(c + 1) * 128], identity=identb[:]),
                         reads=[R("mix_A"), R("mix_B"), R("mix_C"), R("identb")], writes=[RB[bt]], sig=(c == 7))
                P.op("dve", lambda e: e.tensor_copy(out=mixT[:].rearrange("p c t -> p (c t)"), in_=mtb), reads=[RB[bt]], writes=[R("mixT")])

            def post2c(by=0):
                for hf in range(2):
                    for k in range(8):
                        P.op("pe", lambda e, hf=hf, k=k: e.matmul(out=bank(by + hf), lhsT=mixT[:, k, :], rhs=wout[:, k, hf * 512:(hf + 1) * 512], start=(k == 0), stop=(k == 7)),
                             reads=[R("mixT"), WOUT[k]], writes=[RB[by + hf]], sig=(k == 7))
                yt = S1[:, 0:1024]
                ytr = [R("S1_QA"), R("S1_KA"), R("S1_KB"), R("S1_QB")]
                P.op("dve", lambda e: e.tensor_tensor(out=yt, in0=PS[:, by * 512:by * 512 + 1024], in1=gb[:], op=ALU.mult), reads=[RB[by], RB[by + 1], R("gb")], writes=ytr)
                P.op("pool", lambda e: e.tensor_tensor(out=xt, in0=xt, in1=yt, op=ALU.add), reads=ytr + [xres], writes=[xres])
                if out_ap is not None:
                    P.dma("sp", lambda e: e.dma_start(out=out_ap, in_=xt), reads=[xres], writes=[out_res])

            def post2():
                post2a(2)
                post2b(3)
                post2c(0)

            return (loop, post1, post2, post2a, post2b, post2c)

        def run_tiles(items, inj=None, first_inject=None, defer_last=False):
            n = len(items)
            if first_inject is not None:
                items[0][0](first_inject)
            else:
                items[0][0]()
            items[0][1]()
            for i in range(1, n):
                if inj is None:
                    items[i][0]()
                    items[i - 1][2]()
                else:
                    pa, pb, pc = items[i - 1][3], items[i - 1][4], items[i - 1][5]
                    items[i][0]({inj[0]: (lambda pa=pa: pa(6)), inj[1]: (lambda pb=pb: pb(7)), inj[2]: (lambda pc=pc: pc(6))})
                items[i][1]()
            if defer_last:
                return items[n - 1][3:6]
            items[n - 1][2]()
            return None

        def fourier_prompt():
            for c in range(2):
                for s2 in range(2):
                    for zw in range(2):
                        P.op("pe", lambda e, c=c, s2=s2, zw=zw: e.matmul(out=bank(2, c * 256, 256), lhsT=ZW[:, s2, c * 256 + zw * 128:c * 256 + (zw + 1) * 128],
                                                                         rhs=dftp[:, s2, zw, :], start=(s2 == 0 and zw == 0), stop=(s2 == 1 and zw == 1)),
                             reads=[ZWR[s2], R("dftp")], writes=[RB[2]], sig=(s2 == 1 and zw == 1))
            P.op("act", lambda e: e.copy(out=FT[:, :, 0:256], in_=bank(2).rearrange("p (c s) -> p c s", c=2)), reads=[RB[2]], writes=[R("FT")])

        def fourier_inject(blk):
            def sdma(s2):
                slot = s2 % 4
                rv = ring[:, slot, :].rearrange("p (z s) -> p z s", z=2)
                P.dma("sp", lambda e: e.dma_start(out=rv, in_=d_dfts[s2].rearrange("p (z s) -> p z s", z=2)[:, :, blk * 512:(blk + 1) * 512]), writes=[RING[slot]])

            def stile(s2):
                slot = s2 % 4
                rv = ring[:, slot, :].rearrange("p (z s) -> p z s", z=2)
                for c in range(2):
                    for zw in range(2):
                        P.op("pe", lambda e, c=c, zw=zw: e.matmul(out=bank(6 + c), lhsT=ZW[:, s2, c * 256 + zw * 128:c * 256 + (zw + 1) * 128],
                                                                  rhs=rv[:, zw, :], start=(s2 == 0 and zw == 0),
                                                                  stop=(s2 == 7 and zw == 1)),
                             reads=[ZWR[s2], RING[slot]], writes=[RB[6 + c]], sig=(zw == 1))
                if s2 + 4 < 8:
                    sdma(s2 + 4)

            for s2_ in range(4):
                sdma(s2_)

            def evac():
                for c in range(2):
                    P.op("dve", lambda e, c=c: e.tensor_copy(out=FT[:, c, :], in_=bank(6 + c)), reads=[RB[6 + c]], writes=[R("FT")])
            d = {i: (lambda i=i: stile(i)) for i in range(8)}
            d[8] = evac
            return d

        def run_p1(tiles, after_a=None, final_hook=None, pre_hook=None, mid_hook=None):
            T = tiles
            n = len(T)
            hook = after_a if after_a else (lambda i: None)
            T[0]["F"]()
            T[0]["X"]()
            if pre_hook:
                pre_hook()
            if n > 1:
                T[1]["F"]()
            T[0]["Y"]()
            T[0]["Bx"]()
            hook(0)
            for t in range(1, n):
                T[t]["X"]()
                T[t - 1]["By"]()
                T[t - 1]["B2a"]()
                if mid_hook:
                    mid_hook(t - 1)
                if t + 1 < n:
                    T[t + 1]["F"]()
                T[t]["Y"]()
                T[t]["Bx"]()
                T[t - 1]["B2b"]()
                hook(t)
            T[n - 1]["By"]()
            T[n - 1]["B2a"]()
            if final_hook:
                final_hook()
            T[n - 1]["B2b"]()

        sstore = dict(qta=sQTA, qtb=sQTB, kta=sKTA, ktb=sKTB, va=sVA, vb=sVB,
                      qta_r=ARENA_S[0], qtb_r=ARENA_S[1], kta_r=ARENA_S[2], ktb_r=ARENA_S[3], va_r=ARENA_S[4], vb_r=ARENA_S[5])
        pstore = dict(qta=pQTA, qtb=pQTB, kta=pKTA, ktb=pKTB, va=pVA, vb=pVB,
                      qta_r=R("pQTA"), qtb_r=R("pQTB"), kta_r=R("pKTA"), ktb_r=R("pKTB"), va_r=R("pVA"), vb_r=R("pVB"))

        def sample_dst(t, qslot):
            d = dict(sg=qslot, zw=t, va=sVA[:, t, :, :], vb=sVB[:, t, :, :], va_r=ARENA_S[4], vb_r=ARENA_S[5],
                     kta=sKTA[:, t * 128:(t + 1) * 128], ktb=sKTB[:, :, t * 128:(t + 1) * 128], kta_r=ARENA_S[2], ktb_r=ARENA_S[3],
                     qta_r=ARENA_S[0], qtb_r=ARENA_S[1])
            if qslot is not None:
                d["qta"] = sQTA[:, :, qslot * 128:(qslot + 1) * 128]
                d["qtb"] = sQTB[:, :, :, qslot * 128:(qslot + 1) * 128]
            return d

        def sample_layer(l, nblk):
            P.op("pool", lambda e: e.memset(sVA, 1.0), reads=XP, writes=ARENA_S + XP + ZWR)
            P.op("pool", lambda e: e.memset(sVB, 1.0), writes=[ARENA_S[5]])
            for g in range(2):
                P.dma("pool", lambda e, g=g: e.dma_start(out=sVA[:, 8:12, g, 0:64], in_=d_ca[l][:, 128 + g * 64:192 + g * 64].rearrange("(c p) d -> p c d", p=128)),
                      reads=[ARENA_S[4]], writes=[R("sVAc%d" % g)])
            for g in range(4):
                P.dma("pool", lambda e, g=g: e.dma_start(out=sVB[:, 8:12, g, 0:64], in_=d_cb[l][:, 256 + g * 64:320 + g * 64].rearrange("(c p) d -> p c d", p=128)),
                      reads=[ARENA_S[5]], writes=[R("sVBc%d" % g)])
            P.dma("pool", lambda e: e.dma_start(out=sCK[:, :, 0:128], in_=d_ca[l][:, 0:128].rearrange("(c p) d -> p c d", p=128)), reads=[ARENA_S[6]], writes=[R("ckA")])
            P.dma("pool", lambda e: e.dma_start(out=sCK[:, :, 128:384], in_=d_cb[l][:, 0:256].rearrange("(c p) d -> p c d", p=128)), reads=[ARENA_S[6]], writes=[R("ckB")])
            P.op("pool", lambda e: e.memset(ssb[:, 1:2], 0.0), reads=[R("sVAc%d" % c_) for c_ in range(2)] + [R("sVBc%d" % c_) for c_ in range(4)],
                 writes=[ARENA_S[4], ARENA_S[5], R("ssb")])

            def cache_chunk(c):
                tb = bank_bf(1)
                for j in range(3):
                    P.op("pe", lambda e, j=j: e.transpose(out=tb[:, j * 128:(j + 1) * 128], in_=sCK[:, c, j * 128:(j + 1) * 128], identity=identb[:]),
                         reads=[R("ckA"), R("ckB"), R("identb")], writes=[RB[1]], sig=(j == 2))
                P.op("dve", lambda e: e.tensor_copy(out=sKTA[:, 1024 + c * 128:1024 + (c + 1) * 128], in_=tb[:, 0:128]), reads=[RB[1]], writes=[ARENA_S[2]])
                P.op("dve", lambda e: e.tensor_copy(out=sKTB[:, :, 1024 + c * 128:1024 + (c + 1) * 128], in_=tb[:, 128:384].rearrange("p (a t) -> p a t", a=2)),
                     reads=[RB[1]], writes=[ARENA_S[3]])
                if c == 3:
                    P.op("pool", lambda e: e.memset(ssb[:, 3:4], 0.0), reads=[R("ckA"), R("ckB")], writes=[ARENA_S[6], R("ssb")])
            P.mark("s_cache")
            sc0 = 0 if l == 0 else 16
            if l == 0:
                xstats([xs[:, t, :] for t in range(8)], XS, 0)
            P.mark("s_xstats")
            tiles = []
            for t in range(8):
                half, tt = t // 4, t % 4
                d_ = phase1(l, 1, xs[:, t, :], XS[t], sc0 + t, "full" if half == 0 else "kvu", sample_dst(t, tt if half == 0 else None), ropev=rope[:, tt, :], par=t % 2)
                if tt == 0:
                    def b2a_(f=d_["B2a"], half=half):
                        P.dma("sp", lambda e: e.dma_start(out=rope[:], in_=d_rope[half * 4:(half + 1) * 4].rearrange("t p n -> p t n")), writes=[R("rope")])
                        f()
                    d_["B2a"] = b2a_
                tiles.append(d_)
            def after_a(i):
                if i < 4:
                    cache_chunk(i)
                if l == 0 and i == 2:
                    load_wout(0)
                if l == 0 and i < 4:
                    mod_stream([(0, 16 + 2 * i), (0, 17 + 2 * i)])
                if l == 0 and i == 4:
                    mod_flush()
                if i == 5:
                    build_gate(l, 1)
            run_p1(tiles, after_a)
            P.mark("s_p1")
            for blk in range(nblk):
                if blk == 1:
                    tiles = []
                    for tt in range(4):
                        t = 4 + tt
                        tiles.append(phase1(l, 1, xs[:, t, :], XS[t], sc0 + t, "qg", sample_dst(t, tt), ropev=rope[:, tt, :], par=tt % 2))
                    dpa, dpb, dpc = deferred

                    def hk(i):
                        if i == 0:
                            dpb(1)
                            dpc(0)
                    run_p1(tiles, after_a=hk, pre_hook=lambda: dpa(0))
                if l == 0 and blk == 0:
                    precast_win1()
                items = []
                for tt in range(4):
                    t = blk * 4 + tt
                    last = (l == 1)
                    items.append(attend(l, sstore, slice(tt * 128, (tt + 1) * 128), 12, tt, slice(tt * 128, (tt + 1) * 128), xs[:, t, :], XS[t],
                                        o_ys[t * 128:(t + 1) * 128, :] if last else None, R("o_ys")))
                deferred = run_tiles(items, inj=(0, 6, 7), first_inject=fourier_inject(blk), defer_last=(nblk == 2 and blk == 0))
                P.mark("s_att_%d" % blk)

        def prompt_layer(l, after_p1=None):
            build_gate(l, 0)
            P.op("pool", lambda e: e.memset(ssb[:, 0:1], 0.0), writes=ZWR + [R("ssb")])
            if l == 0:
                for t in range(8):
                    P.dma("sp", lambda e, t=t: e.dma_start(out=xp[:, t, :], in_=d_xp[t * 128:(t + 1) * 128, :]), reads=[], writes=[XP[t]] + ARENA_S)
                P.op("pool", lambda e: e.memset(pVA[:], 1.0), writes=[pstore["va_r"]])
                P.op("pool", lambda e: e.memset(pVB[:], 1.0), writes=[pstore["vb_r"]])
            xstats([xp[:, t, :] for t in range(2)], XP[0:2], 8)
            pend = {}
            for s in range(4):
                tiles = []
                for ti in range(2):
                    t = s * 2 + ti
                    d = dict(sg=ti, zw=ti, va=pVA[:, ti, :, :], vb=pVB[:, ti, :, :], va_r=pstore["va_r"], vb_r=pstore["vb_r"],
                             kta=pKTA[:, ti * 128:(ti + 1) * 128], ktb=pKTB[:, :, ti * 128:(ti + 1) * 128], kta_r=ZWR[6], ktb_r=ZWR[7],
                             qta=pQTA[:, :, ti * 128:(ti + 1) * 128], qtb=pQTB[:, :, :, ti * 128:(ti + 1) * 128], qta_r=ZWR[2], qtb_r=ZWR[4])
                    tiles.append(phase1(l, 0, xp[:, t, :], XP[t], 8 + t, "full", d, ropev=None,
                                        outs=(o_na[s, l, ti * 128:(ti + 1) * 128, :], o_nb[s, l, ti * 128:(ti + 1) * 128, :]), par=ti))
                def after_a(i, s=s):
                    if i == 0 and s < 3:
                        xstats([xp[:, t, :] for t in range(2 * s + 2, 2 * s + 4)], XP[2 * s + 2:2 * s + 4], 8 + 2 * s + 2)
                    if l == 0:
                        tix = s * 2 + i
                        mod_stream([(1, bb) for bb in range(tix * 3, tix * 3 + 3)])
                        if tix == 7:
                            mod_flush()
                    if i == 0:
                        for f in pend.pop(0, []):
                            f()

                def final_hook():
                    for f in pend.pop(1, []):
                        f()
                def pre_hook():
                    for f in pend.pop("pre", []):
                        f()
                def mid_hook(i):
                    for f in pend.pop("mid%d" % i, []):
                        f()
                run_p1(tiles, after_a, final_hook, pre_hook, mid_hook)
                if s == 3 and after_p1 is not None:
                    after_p1()
                fourier_prompt()
                pst = dict(pstore)
                pst.update(qta_r=ZWR[2], qtb_r=ZWR[4], kta_r=ZWR[6], ktb_r=ZWR[7])
                items = []
                for ti in range(2):
                    t = s * 2 + ti
                    items.append(attend(l, pst, slice(ti * 128, (ti + 1) * 128), 2, ti, slice(ti * 128, (ti + 1) * 128), xp[:, t, :], XP[t],
                                        o_yp[t * 128:(t + 1) * 128, :] if l == 1 else None, R("o_yp"), alt=(ti == 1)))
                items[0][0]()
                items[0][1]()
                items[1][0]()
                items[1][1]()
                i0_, i1_ = items
                pend["pre"] = [lambda f=i0_[3]: f(0)]
                pend[0] = [lambda f=i0_[4]: f(1), lambda f=i0_[5]: f(0)]
                pend["mid0"] = [lambda f=i1_[3]: f(0)]
                pend[1] = [lambda f=i1_[4]: f(1), lambda f=i1_[5]: f(0)]
            for i in ("pre", 0, "mid0", 1):
                for f in pend.pop(i, []):
                    f()

        P.mark("setup")
        mod_super(0, 0)
        load_win(0, parts=tuple(range(8)))
        mod_super(0, 1)
        load_win(0, parts=tuple(range(8, 16)))
        P.mark("mod0")
        P.mark("weights0")
        sample_layer(0, 2)
        xstats([xs[:, t, :] for t in range(8)], XS, 16)
        P.mark("sample0")
        prompt_layer(0, after_p1=load_win_fast)
        load_wout(1)
        prompt_layer(1)
        sample_layer(1, 1)
        P.finish([R("o_ys"), R("o_yp"), R("o_na"), R("o_nb")])
        P.build(nc, st)
        print("n_inst", P.n_inst, {n: len(e.prog) for n, e in P.e.items()})
    return nc


def _bf16(a):
    return np.asarray(a, dtype=np.float32).astype(ml_dtypes.bfloat16)


def _consts():
    identf = np.eye(128, dtype=np.float32)
    ch = np.arange(64)
    ang = 2 * np.pi * np.outer(ch, ch) / 64.0
    c64, s64 = np.cos(ang) / 8.0, np.sin(ang) / 8.0
    bcs = np.zeros((128, 256), np.float64)
    for g in range(2):
        bcs[g * 64:(g + 1) * 64, g * 64:(g + 1) * 64] = c64
        bcs[g * 64:(g + 1) * 64, 128 + g * 64:128 + (g + 1) * 64] = s64
    sp = np.arange(256)
    a = 2 * np.pi * np.outer(sp, sp) / 256.0
    cp, sn = np.cos(a) / 16.0, -np.sin(a) / 16.0
    dftp = np.stack([np.stack([cp[t * 128:(t + 1) * 128], sn[t * 128:(t + 1) * 128]], 1) for t in range(2)], 1)
    masks = np.zeros((128, 2), np.float32)
    masks[:, 0] = ((np.arange(128) % 64) < 32)
    masks[:, 1] = 1.0 - masks[:, 0]
    return identf, _bf16(identf), _bf16(bcs), _bf16(dftp.reshape(128, 1024)), masks


def _sample_tables(par):
    pos = np.concatenate([np.arange(512) + 512 * par, np.arange(512) + 512 * (1 - par)])
    a = 2 * np.pi * (np.outer(pos, pos) % 1024) / 1024.0
    c, s = np.cos(a) / 32.0, -np.sin(a) / 32.0
    dfts = np.stack([c, s], 1).reshape(8, 128, 2048)
    row = (pos // 64).astype(np.float64)
    col = (pos % 64).astype(np.float64)
    tabs = []
    for dim in (64, 32):
        nf = dim // 4
        inv = 1.0 / (10000.0 ** (np.arange(nf, dtype=np.float64) / nf))
        inv = inv.astype(np.float32).astype(np.float64)
        ar = (row[:, None].astype(np.float32) * inv.astype(np.float32)).astype(np.float64)
        ac = (col[:, None].astype(np.float32) * inv.astype(np.float32)).astype(np.float64)
        cosv = np.concatenate([np.cos(ar), np.cos(ar), np.cos(ac), np.cos(ac)], 1)
        sinv = np.concatenate([-np.sin(ar), np.sin(ar), -np.sin(ac), np.sin(ac)], 1)
        tabs += [cosv, sinv]
    rope = np.concatenate(tabs, 1).astype(np.float32).reshape(8, 128, 192)
    return _bf16(dfts), rope


_NC_CACHE = {}


def _prep(x_prompt, x_sample, cache_attn_a, cache_attn_b, c, c_ctx, norm_g, w_mod, b_mod, w_in,
          q_norm_a, k_norm_a, q_norm_b, k_norm_b, lambda_q1, lambda_k1, lambda_q2, lambda_k2,
          subln_g, w_fourier, w_out):
    f = lambda a: np.ascontiguousarray(np.asarray(a, dtype=np.float32))
    x_prompt, x_sample, cache_attn_a, cache_attn_b = f(x_prompt), f(x_sample), f(cache_attn_a), f(cache_attn_b)
    c, c_ctx, norm_g, w_mod, b_mod, w_in = f(c), f(c_ctx), f(norm_g), f(w_mod), f(b_mod), f(w_in)
    w_fourier, w_out = f(w_fourier), f(w_out)
    identf, identb, bcs, dftp, masks = _consts()
    tabs = [_sample_tables(0), _sample_tables(1)]
    smallv = np.concatenate([f(q_norm_a), f(k_norm_a), f(k_norm_b), f(q_norm_b), f(subln_g),
                             f(lambda_q1), f(lambda_k1), f(lambda_q2), f(lambda_k2)], axis=1)
    bmodT = np.ascontiguousarray(b_mod.reshape(2, 24, 128).transpose(2, 0, 1).reshape(128, 48))
    normgT = np.ascontiguousarray(norm_g.reshape(2, 8, 128).transpose(2, 0, 1).reshape(128, 16))
    in_maps = []
    for i in range(8):
        b, par = i // 2, i % 2
        xs = np.concatenate([x_sample[b, 512 * par:512 * par + 512], x_sample[b, 512 * (1 - par):512 * (1 - par) + 512]], 0)
        condT = np.stack([c_ctx.reshape(8, 128).T, c[b].reshape(8, 128).T], axis=2).reshape(128, 16)
        in_maps.append(dict(
            xp=np.ascontiguousarray(x_prompt[4 * i:4 * i + 4].reshape(1024, 1024)), xs=np.ascontiguousarray(xs),
            ca=np.ascontiguousarray(cache_attn_a[b].reshape(2, 512, 256)), cb=np.ascontiguousarray(cache_attn_b[b].reshape(2, 512, 512)),
            condT=np.ascontiguousarray(condT), bmodT=bmodT, normgT=normgT, smallv=smallv,
            w_mod=w_mod, w_in=w_in, w_out=w_out, w_fourier=w_fourier,
            identf=identf, identb=identb, bcs=bcs, dftp=dftp, dfts=tabs[par][0], rope=tabs[par][1], masks=masks))
    return in_maps


def kernel(**inputs):
    in_maps = _prep(**inputs)
    if "nc" not in _NC_CACHE:
        _NC_CACHE["nc"] = build_program()
    nc = _NC_CACHE["nc"]
    res = run_bass_kernel_spmd(nc, in_maps, core_ids=list(range(8)))
    y_prompt = np.zeros((32, 256, 1024), np.float32)
    y_sample = np.zeros((4, 1024, 1024), np.float32)
    new_a = np.zeros((32, 2, 256, 2, 2, 64), np.float32)
    new_b = np.zeros((32, 2, 256, 2, 4, 64), np.float32)
    for i in range(8):
        r = res.results[i]
        b, par = i // 2, i % 2
        y_prompt[4 * i:4 * i + 4] = r["yp"].reshape(4, 256, 1024)
        y_sample[b, 512 * par:512 * par + 512] = r["ys"]
        new_a[4 * i:4 * i + 4] = r["na"].reshape(4, 2, 256, 2, 2, 64)
        new_b[4 * i:4 * i + 4] = r["nb"].reshape(4, 2, 256, 2, 4, 64)
    return (y_prompt, y_sample, new_a, new_b)
```

```python
import math
import os
from contextlib import ExitStack
import numpy as np
import ml_dtypes
import concourse.bass as bass
import concourse.mybir as mybir
from concourse.bass_utils import run_bass_kernel_spmd

F32 = mybir.dt.float32
BF16 = mybir.dt.bfloat16
ALU = mybir.AluOpType
POWOP = ALU.pow
AF = mybir.ActivationFunctionType
AX = mybir.AxisListType
EPS = 1e-6


class Res:
    __slots__ = ("name", "w", "r", "excl")

    def __init__(self, name):
        self.name = name
        self.w = []
        self.r = {}
        self.excl = name.startswith("bank")


class Eng:
    def __init__(self, name, is_pe=False):
        self.name = name
        self.is_pe = is_pe
        self.tick = 0
        self.obs = {}
        self.prog = []
        self.pending = False


class Prog:
    ENGS = ("pe", "act", "dve", "pool", "sp")

    def __init__(self, n_dma_sems=10):
        self.e = {n: Eng(n, n == "pe") for n in self.ENGS}
        self.n_dma_sems = n_dma_sems
        self.dma_cnt = {}
        self.dma_rr = {n: 0 for n in self.ENGS}
        self.n_inst = 0
        self.disabled = False
        self.marks = 0
        self.limit = int(os.environ.get("KSTOP", "0"))
        self.oplimit = int(os.environ.get("KOPS", "0"))

    def mark(self, name=""):
        self.marks += 1
        if self.limit and self.marks >= self.limit and not self.disabled:
            self.disabled = True
            print("KSTOP at mark", self.marks, name)

    def _deps(self, eng, reads, writes):
        deps = {}
        own = "t_" + eng.name
        for r in reads:
            for (k, v) in r.w:
                if deps.get(k, 0) < v:
                    deps[k] = v
            if r.excl:
                for k, v in r.r.items():
                    if k != own and deps.get(k, 0) < v:
                        deps[k] = v
        for w in writes:
            for (k, v) in w.w:
                if deps.get(k, 0) < v:
                    deps[k] = v
            for k, v in w.r.items():
                if deps.get(k, 0) < v:
                    deps[k] = v
        for k, v in deps.items():
            if eng.is_pe and k == "t_pe":
                continue
            if eng.obs.get(k, 0) < v:
                eng.prog.append(("wait", k, v))
                eng.obs[k] = v

    def op(self, engname, fn, reads=(), writes=(), sig=True):
        if self.oplimit:
            sig = True
            if self.n_inst >= self.oplimit:
                self.disabled = True
        if self.disabled:
            return
        eng = self.e[engname]
        self._deps(eng, reads, writes)
        key = "t_" + engname
        ev = (key, eng.tick + 1)
        if sig:
            eng.tick += 1
            eng.pending = False
        else:
            eng.pending = True
        eng.prog.append(("op", fn, key if sig else None))
        for r in reads:
            if r.r.get(key, 0) < ev[1]:
                r.r[key] = ev[1]
        for w in writes:
            w.w = [ev]
            w.r = {}
        self.n_inst += 1

    def dma(self, engname, fn, reads=(), writes=()):
        if self.oplimit and self.n_inst >= self.oplimit:
            self.disabled = True
        if self.disabled:
            return
        eng = self.e[engname]
        self._deps(eng, reads, writes)
        i = self.dma_rr[engname]
        self.dma_rr[engname] = (i + 1) % self.n_dma_sems
        key = "d_%s_%d" % (engname, i)
        prev = self.dma_cnt.get(key, 0)
        if prev and eng.obs.get(key, 0) < prev:
            eng.prog.append(("wait", key, prev))
            eng.obs[key] = prev
        val = prev + 16
        self.dma_cnt[key] = val
        eng.prog.append(("dma", fn, key))
        for r in reads:
            if r.r.get(key, 0) < val:
                r.r[key] = val
        for w in writes:
            w.w = [(key, val)]
            w.r = {}
        self.n_inst += 1

    def finish(self, out_res):
        eng = self.e["sp"]
        self._deps(eng, out_res, ())
        for n, e in self.e.items():
            assert not e.pending, "engine %s has unsignalled trailing ops" % n

    def build(self, nc, stack):
        sems = {}
        keys = ["t_" + n for n in self.ENGS] + sorted(self.dma_cnt.keys())
        for k in keys:
            sems[k] = stack.enter_context(nc.semaphore(k))
        block = stack.enter_context(nc.Block())

        def replay(name):
            def body(h):
                for item in self.e[name].prog:
                    if item[0] == "wait":
                        h.wait_ge(sems[item[1]], item[2])
                    elif item[0] == "op":
                        ins = item[1](h)
                        if item[2] is not None:
                            ins.then_inc(sems[item[2]], 1)
                    else:
                        ins = item[1](h)
                        ins.then_inc(sems[item[2]], 16)
            return body

        block.tensor(replay("pe"))
        block.scalar(replay("act"))
        block.vector(replay("dve"))
        block.gpsimd(replay("pool"))
        block.sync(replay("sp"))


LAM_INIT = [0.8 - 0.6 * math.exp(-0.3 * l) for l in range(2)]


def build_program():
    nc = bass.Bass("TRN2", target_bir_lowering=False)
    di = lambda name, shape, dt=F32: nc.dram_tensor(name, shape, dt, kind="ExternalInput").ap()
    do = lambda name, shape: nc.dram_tensor(name, shape, F32, kind="ExternalOutput").ap()
    d_xp = di("xp", [1024, 1024])
    d_xs = di("xs", [1024, 1024])
    d_ca = di("ca", [2, 512, 256])
    d_cb = di("cb", [2, 512, 512])
    d_condT = di("condT", [128, 16])
    d_bmodT = di("bmodT", [128, 48])
    d_normgT = di("normgT", [128, 16])
    d_smallv = di("smallv", [2, 384])
    d_wmod = di("w_mod", [2, 1024, 3072])
    d_win = di("w_in", [2, 1024, 2816])
    d_wout = di("w_out", [2, 1024, 1024])
    d_wc = di("w_fourier", [2, 256, 256])
    d_identf = di("identf", [128, 128])
    d_identb = di("identb", [128, 128], BF16)
    d_bcs = di("bcs", [128, 256], BF16)
    d_dftp = di("dftp", [128, 1024], BF16)
    d_dfts = di("dfts", [8, 128, 2048], BF16)
    d_rope = di("rope", [8, 128, 192])
    d_masks = di("masks", [128, 2])
    o_yp = do("yp", [1024, 1024])
    o_ys = do("ys", [512, 1024])
    o_na = do("na", [4, 2, 256, 256])
    o_nb = do("nb", [4, 2, 256, 512])

    st = ExitStack()
    with st:
        sb = lambda name, shape, dt: st.enter_context(nc.sbuf_tensor("sb_" + name, shape, dt))
        P = Prog()
        _res = {}

        def R(name):
            if name not in _res:
                _res[name] = Res(name)
            return _res[name]

        xs = sb("xs", [128, 8, 1024], F32)
        arena = sb("arena", [128, 8192], F32)
        xp = arena[:, :].rearrange("p (t n) -> p t n", t=8)
        abf = arena[:, :].bitcast(BF16)
        sQTA = abf[:, 0:2048].rearrange("p (j t) -> p j t", j=4)
        sQTB = abf[:, 2048:4096].rearrange("p (a c t) -> p a c t", a=2, c=2)
        sKTA = abf[:, 4096:5632]
        sKTB = abf[:, 5632:8704].rearrange("p (a t) -> p a t", a=2)
        sVA = abf[:, 8704:10264].rearrange("p (c g e) -> p c g e", c=12, g=2)
        sVB = abf[:, 10264:13384].rearrange("p (c g e) -> p c g e", c=12, g=4)
        sCK = abf[:, 13384:14920].rearrange("p (c n) -> p c n", c=4)
        win = sb("win", [128, 8, 2816], BF16)
        wout = sb("wout", [128, 8, 1024], BF16)
        wc = sb("wc", [128, 2, 2, 256], BF16)
        ring = sb("ring", [128, 4, 1024], BF16)
        SG = sb("SG", [128, 4, 1024], BF16)
        ZW = sb("ZW", [128, 8, 512], BF16)
        zwf = ZW[:, :, :].rearrange("p a b -> p (a b)")
        pQTA = zwf[:, 1024:2048].rearrange("p (j t) -> p j t", j=4)
        pQTB = zwf[:, 2048:3072].rearrange("p (a c t) -> p a c t", a=2, c=2)
        pKTA = zwf[:, 3072:3328]
        pKTB = zwf[:, 3584:4096].rearrange("p (a t) -> p a t", a=2)
        pVA = sb("pVA", [128, 2, 2, 65], BF16)
        pVB = sb("pVB", [128, 2, 4, 65], BF16)
        UT = sb("UT", [128, 256], BF16)
        FT = sb("FT", [128, 2, 512], BF16)
        identf = sb("identf_s", [128, 128], F32)
        identb = sb("identb_s", [128, 128], BF16)
        onesf = sb("onesf", [128, 128], F32)
        bcs = sb("bcs_s", [128, 256], BF16)
        dftp = sb("dftp_s", [128, 2, 2, 256], BF16)
        rope = sb("rope_s", [128, 4, 192], F32)
        smallv = sb("smallv_s", [128, 2, 384], F32)
        masks = sb("masks_s", [128, 2], F32)
        cm05 = sb("cm05", [128, 32], F32)
        condT = sb("condT_s", [128, 8, 2], F32)
        scT = sb("scT", [128, 8, 2], F32)
        scTb = sb("scTb", [128, 8, 2], BF16)
        bmodT = sb("bmodT_s", [128, 2, 24], F32)
        normgT = sb("normgT_s", [128, 2, 8], F32)
        modT = sb("modT", [128, 2, 24, 2], F32)
        gsT = sb("gsT", [128, 2, 8, 2], F32)
        lamn = sb("lamn", [128, 2], F32)
        lamt = sb("lamt", [128, 8], F32)
        ssx = sb("ssx", [128, 24], F32)
        rstdx = sb("rstdx", [128, 24], F32)
        ssq = sb("ssq", [128, 32], F32)
        rq = sb("rq", [128, 32], F32)
        recA = sb("recA", [128, 8], F32)
        recB = sb("recB", [128, 8], F32)
        ssb = sb("ssb", [128, 8], F32)
        gb = sb("gb", [128, 1024], F32)
        dg = sb("dg", [128, 2, 128], F32)
        thb = sb("thb", [128, 1024], BF16)
        hT2 = sb("hT", [128, 2, 8, 128], BF16)
        S1 = sb("S1", [128, 1152], F32)
        S2 = sb("S2", [128, 1152], F32)
        S3 = sb("S3", [128, 1152], F32)
        qkb = sb("qkb", [128, 1152], BF16)
        S4 = sb("S4", [128, 1152], F32)
        ostage = S4[:, 0:768]
        PT = sb("PT", [128, 2, 1024], BF16)
        mix = sb("mix", [128, 1024], BF16)
        mixT = sb("mixT", [128, 8, 128], BF16)
        obt = sb("obt", [128, 256], F32)[:, :]
        sqb = sb("sqb", [128, 256], F32)[:, :]
        PS = st.enter_context(nc.psum_tensor("PS", [128, 4096], F32))
        RB = [R("bank%d" % b) for b in range(8)]

        def bank(b, c0=0, n=512):
            return PS[:, b * 512 + c0:b * 512 + c0 + n]

        def bank_bf(b, c0=0, n=1024):
            return PS[:, b * 512:(b + 1) * 512].bitcast(BF16)[:, c0:c0 + n]

        ARENA_S = [R(n) for n in ("sQTA", "sQTB", "sKTA", "sKTB", "sVA", "sVB", "sCK")]
        XP = [R("xp%d" % t) for t in range(8)]
        XS = [R("xs%d" % t) for t in range(8)]
        ZWR = [R("zw%d" % t) for t in range(8)]
        SGR = [R("sg%d" % t) for t in range(4)]
        HT2 = [[R("hT%d_%d" % (p_, c)) for c in range(8)] for p_ in range(2)]
        WIN = [[R("win%d_%d" % (k, h)) for h in range(2)] for k in range(8)]
        RING = [R("ring%d" % i) for i in range(4)]
        PTR = [R("pt0"), R("pt1")]
        WOUT = [R("wout%d" % k) for k in range(8)]
        OST = [R("ost_KA"), R("ost_VA"), R("ost_KB"), R("ost_VB")]

        def ld(eng, dst, src, w, r=()):
            P.dma(eng, lambda e: e.dma_start(out=dst, in_=src), reads=list(r), writes=list(w))

        ld("sp", identf[:], d_identf[:, :], [R("identf")])
        ld("sp", identb[:], d_identb[:, :], [R("identb")])
        ld("sp", condT[:].rearrange("p a b -> p (a b)"), d_condT[:, :], [R("condT")])
        ld("sp", bmodT[:].rearrange("p a b -> p (a b)"), d_bmodT[:, :], [R("bmodT")])
        ld("sp", normgT[:].rearrange("p a b -> p (a b)"), d_normgT[:, :], [R("normgT")])
        ld("sp", smallv[:], d_smallv.partition_broadcast(128), [R("smallv")])
        ld("sp", masks[:], d_masks[:, :], [R("masks")])
        ld("sp", bcs[:], d_bcs[:, :], [R("bcs")])
        ld("sp", dftp[:].rearrange("p a b c -> p (a b c)"), d_dftp[:, :], [R("dftp")])
        for t in range(8):
            ld("act", xs[:, t, :], d_xs[t * 128:(t + 1) * 128, :], [XS[t]])
        P.op("pool", lambda e: e.memset(onesf[:], 1.0), writes=[R("onesf")])
        P.op("pool", lambda e: e.memset(cm05[:], -0.5), writes=[R("cm05")])
        P.op("dve", lambda e: e.memset(ssq[:], 1.0), writes=[R("ssq_QA"), R("ssq_KA"), R("ssq_KB"), R("ssq_QB")])
        for l in range(2):
            P.dma("pool", lambda e, l=l: e.dma_start(out=wc[:, l, :, :], in_=d_wc[l].rearrange("(c p) n -> p c n", p=128)),
                  writes=[R("wc")])

        P.op("act", lambda e: e.activation(out=scT[:], in_=condT[:], func=AF.Tanh, scale=0.5), reads=[R("condT")], writes=[R("scT")])
        P.op("dve", lambda e: e.scalar_tensor_tensor(out=scT[:], in0=scT[:], scalar=1.0, in1=condT[:], op0=ALU.add, op1=ALU.mult),
             reads=[R("scT"), R("condT")], writes=[R("scT")])
        P.op("dve", lambda e: e.tensor_scalar(out=scT[:], in0=scT[:], scalar1=0.5, scalar2=None, op0=ALU.mult), reads=[R("scT")], writes=[R("scT")])
        P.op("dve", lambda e: e.tensor_copy(out=scTb[:], in_=scT[:]), reads=[R("scT")], writes=[R("scTb")])
        P.op("dve", lambda e: e.tensor_scalar(out=smallv[:, :, 0:64], in0=smallv[:, :, 0:64], scalar1=64 ** -0.5, scalar2=None, op0=ALU.mult),
             reads=[R("smallv")], writes=[R("smallv")])
        P.op("dve", lambda e: e.tensor_scalar(out=smallv[:, :, 160:192], in0=smallv[:, :, 160:192], scalar1=32 ** -0.5, scalar2=None, op0=ALU.mult),
             reads=[R("smallv")], writes=[R("smallv")])
        for l in range(2):
            P.op("dve", lambda e, l=l: e.tensor_scalar(out=smallv[:, l, 192:256], in0=smallv[:, l, 192:256], scalar1=1.0 - LAM_INIT[l], scalar2=None, op0=ALU.mult),
                 reads=[R("smallv")], writes=[R("smallv")])
        for l in range(2):
            P.op("dve", lambda e, l=l: e.tensor_tensor(out=S1[:, 0:32], in0=smallv[:, l, 256:288], in1=smallv[:, l, 288:320], op=ALU.mult),
                 reads=[R("smallv")], writes=[R("S1_QA")])
            P.op("dve", lambda e, l=l: e.tensor_tensor(out=S1[:, 32:64], in0=smallv[:, l, 320:352], in1=smallv[:, l, 352:384], op=ALU.mult),
                 reads=[R("smallv")], writes=[R("S1_QA")])
            P.op("dve", lambda e, l=l: e.tensor_reduce(out=lamt[:, 0:2], in_=S1[:, 0:64].rearrange("p (a b) -> p a b", a=2), axis=AX.X, op=ALU.add),
                 reads=[R("S1_QA")], writes=[R("lamt")])
            P.op("act", lambda e, l=l: e.activation(out=lamt[:, 2:4], in_=lamt[:, 0:2], func=AF.Exp), reads=[R("lamt")], writes=[R("lamt")])
            P.op("dve", lambda e, l=l: e.tensor_tensor(out=lamt[:, 4:5], in0=lamt[:, 3:4], in1=lamt[:, 2:3], op=ALU.subtract),
                 reads=[R("lamt")], writes=[R("lamt")])
            P.op("dve", lambda e, l=l: e.tensor_scalar(out=lamn[:, l:l + 1], in0=lamt[:, 4:5], scalar1=-LAM_INIT[l], scalar2=None, op0=ALU.add),
                 reads=[R("lamt")], writes=[R("lamn")])

        def load_win(l, parts=None):
            for k in range(8):
                for hlf in range(2):
                    if parts is not None and (k * 2 + hlf) not in parts:
                        continue
                    P.dma("pool", lambda e, l=l, k=k, hlf=hlf: e.dma_start(
                        out=win[:, k, hlf * 1408:(hlf + 1) * 1408], in_=d_win[l][k * 128:(k + 1) * 128, hlf * 1408:(hlf + 1) * 1408]),
                        writes=[WIN[k][hlf]])

        def load_wout(l):
            for k in range(8):
                P.dma("pool", lambda e, l=l, k=k: e.dma_start(out=wout[:, k, :], in_=d_wout[l][k * 128:(k + 1) * 128, :]), writes=[WOUT[k]])

        mod_ctr = [0]
        mod_fifo = []

        def mod_dma(l, blk):
            slot = mod_ctr[0] % 4
            mod_ctr[0] += 1
            wmv = ring[:, slot, :].rearrange("p (k n) -> p k n", k=8)
            P.dma("pool", lambda e: e.dma_start(out=wmv, in_=d_wmod[l][:, blk * 128:(blk + 1) * 128].rearrange("(k p) n -> p k n", p=128)), writes=[RING[slot]])
            mod_fifo.append((l, blk, slot, wmv))

        def mod_mm():
            l, blk, slot, wmv = mod_fifo.pop(0)
            mp = bank(7, 508, 2)
            for k in range(8):
                P.op("pe", lambda e, k=k: e.matmul(out=mp, lhsT=wmv[:, k, :], rhs=scTb[:, k, :], start=(k == 0), stop=(k == 7)),
                     reads=[RING[slot], R("scTb")], writes=[RB[7]], sig=(k == 7))
            P.op("dve", lambda e: e.tensor_scalar(out=modT[:, l, blk, :], in0=mp, scalar1=bmodT[:, l, blk:blk + 1], scalar2=None, op0=ALU.add),
                 reads=[RB[7], R("bmodT")], writes=[R("modT%d" % l)])
            if 8 <= blk < 16:
                c = blk - 8
                P.op("dve", lambda e: e.tensor_scalar(out=gsT[:, l, c, :], in0=modT[:, l, blk, :], scalar1=1.0, scalar2=normgT[:, l, c:c + 1],
                                                      op0=ALU.add, op1=ALU.mult),
                     reads=[R("modT%d" % l), R("normgT")], writes=[R("gsT%d" % l)])

        def mod_super(l, sb_):
            region = bank(7, 496, 16).rearrange("p (c j) -> p c j", j=2)
            slots = []

            def dma(k):
                slot = mod_ctr[0] % 4
                mod_ctr[0] += 1
                P.dma("pool", lambda e: e.dma_start(out=ring[:, slot, :], in_=d_wmod[l][k * 128:(k + 1) * 128, sb_ * 1024:(sb_ + 1) * 1024]), writes=[RING[slot]])
                slots.append(slot)
            for k in range(4):
                dma(k)
            for k in range(8):
                slot = slots[k]
                for cb in range(8):
                    P.op("pe", lambda e, k=k, cb=cb, slot=slot: e.matmul(out=region[:, cb, :], lhsT=ring[:, slot, cb * 128:(cb + 1) * 128], rhs=scTb[:, k, :],
                                                                       start=(k == 0 and cb == 0), stop=(k == 7), skip_group_check=True),
                         reads=[RING[slot], R("scTb")], writes=[RB[7]], sig=(cb == 7))
                if k + 4 < 8:
                    dma(k + 4)
            P.op("dve", lambda e: e.tensor_tensor(out=modT[:, l, sb_ * 8:(sb_ + 1) * 8, :], in0=region,
                                                  in1=bmodT[:, l, sb_ * 8:(sb_ + 1) * 8].unsqueeze(2).to_broadcast([128, 8, 2]), op=ALU.add),
                 reads=[RB[7], R("bmodT")], writes=[R("modT%d" % l)])
            if sb_ == 1:
                P.op("dve", lambda e: e.scalar_tensor_tensor(out=gsT[:, l, :, :], in0=modT[:, l, 8:16, :], scalar=1.0,
                                                             in1=normgT[:, l, :].unsqueeze(2).to_broadcast([128, 8, 2]), op0=ALU.add, op1=ALU.mult),
                     reads=[R("modT%d" % l), R("normgT")], writes=[R("gsT%d" % l)])

        def mod_stream(blocks, depth=3):
            for (l, blk) in blocks:
                mod_dma(l, blk)
                while len(mod_fifo) > depth:
                    mod_mm()

        def mod_flush():
            while mod_fifo:
                mod_mm()

        def build_gate(l, cond):
            for c in range(8):
                P.op("dve", lambda e, c=c: e.tensor_scalar(out=dg[:, c % 2, :], in0=identf[:], scalar1=modT[:, l, 16 + c, cond:cond + 1], scalar2=0.5,
                                                           op0=ALU.mult, op1=ALU.mult),
                     reads=[R("identf"), R("modT%d" % l)], writes=[R("dg%d" % (c % 2))])
                P.op("pe", lambda e, c=c: e.matmul(out=bank(c // 4, (c % 4) * 128, 128), lhsT=onesf[:], rhs=dg[:, c % 2, :], start=True, stop=True),
                     reads=[R("onesf"), R("dg%d" % (c % 2))], writes=[RB[c // 4]])
            P.op("act", lambda e: e.copy(out=gb[:], in_=PS[:, 0:1024]), reads=[RB[0], RB[1]], writes=[R("gb")])

        def xstats(xtiles, xres, c0=0):
            n = len(xtiles)
            sr = [R("ssx%d" % c) for c in range(c0, c0 + n, 2)]
            rr = [R("rstdx%d" % c) for c in range(c0, c0 + n, 2)]
            P.op("dve", lambda e: e.memset(ssx[:, c0:c0 + n], 0.0), writes=sr)
            for i in range(n):
                P.op("act", lambda e, i=i: e.activation(out=thb[:], in_=xtiles[i], func=AF.Square, accum_out=ssx[:, c0 + i:c0 + i + 1]),
                     reads=[xres[i]] + sr, writes=[R("thb0"), R("thb512"), R("thb768")] + sr)
            P.op("dve", lambda e: e.tensor_scalar(out=ssx[:, c0:c0 + n], in0=ssx[:, c0:c0 + n], scalar1=1.0 / 1024, scalar2=EPS, op0=ALU.mult, op1=ALU.add),
                 reads=sr, writes=sr)
            P.op("pool", lambda e: e.tensor_tensor(out=rstdx[:, c0:c0 + n], in0=ssx[:, c0:c0 + n], in1=cm05[:, 0:n], op=POWOP),
                 reads=sr + [R("cm05")], writes=rr)

        def phase1(l, cond, xt, xres, si, mode, dst, ropev=None, outs=None, par=0):
            want_q = mode in ("full", "qg")
            want_k = mode in ("full", "kvu")
            segs = []
            if want_q:
                segs.append(("QA", bank(2), 0, 512, 64, 0, RB[2], 0))
            if want_k:
                segs.append(("KA", bank(3, 0, 128), 512, 128, 64, 8, RB[3], 64))
                segs.append(("KB", bank(5, 256, 256), 640, 256, 32, 10, RB[5], 128))
            if want_q:
                segs.append(("QB", bank(5, 0, 256), 896, 256, 32, 18, RB[5], 160))

            hTv = hT2[:, par, :, :]
            HTp = HT2[par]

            def fF():
                xn = S3[:, 0:1024]
                P.op("act", lambda e: e.activation(out=xn, in_=xt, func=AF.Identity, scale=rstdx[:, si:si + 1]),
                     reads=[xres, R("rstdx%d" % (si - si % 2))], writes=[R("S3")])
                tpx = PS[:, 0:1024].rearrange("p (c t) -> p c t", c=8)
                for c in range(8):
                    P.op("pe", lambda e, c=c: e.transpose(out=tpx[:, c, :], in_=xn[:, c * 128:(c + 1) * 128], identity=identf[:]),
                         reads=[R("S3"), R("identf")], writes=[RB[c // 4]], sig=(c % 4 == 3))
                for c in range(8):
                    P.op("dve", lambda e, c=c: e.tensor_scalar(out=hTv[:, c, :], in0=tpx[:, c, :], scalar1=gsT[:, l, c, cond:cond + 1],
                                                               scalar2=modT[:, l, c, cond:cond + 1], op0=ALU.mult, op1=ALU.add),
                         reads=[RB[c // 4], R("gsT%d" % l), R("modT%d" % l)], writes=[HTp[c]])


            def proj(b, c0, n, oc0=0):
                for k in range(8):
                    P.op("pe", lambda e, k=k: e.matmul(out=bank(b, oc0, n), lhsT=hTv[:, k, :], rhs=win[:, k, c0:c0 + n], start=(k == 0), stop=(k == 7)),
                         reads=[HTp[k]] + WIN[k], writes=[RB[b]], sig=(k == 7))


            def fX():
                if want_q:
                    proj(2, 0, 512)
                if want_k:
                    proj(3, 512, 256)
                if want_k:
                    for ch in range(2):
                        for k in range(8):
                            P.op("pe", lambda e, ch=ch, k=k: e.matmul(out=bank(3, 256 + ch * 128, 128), lhsT=win[:, k, 2304 + ch * 128:2304 + (ch + 1) * 128],
                                                                      rhs=hTv[:, k, :], start=(k == 0), stop=(k == 7)),
                                 reads=[HTp[k]] + WIN[k], writes=[RB[3]], sig=(k == 7))
                if mode == "full":
                    proj(5, 1280, 512)
                elif mode == "kvu":
                    proj(5, 1536, 256, 256)
                else:
                    proj(5, 1280, 256)

            def fY():
                if want_q:
                    proj(4, 768, 512)
                    proj(7, 2560, 256)
                if mode == "full":
                    proj(6, 1792, 512)
                elif mode == "kvu":
                    proj(6, 1792, 256)
                else:
                    proj(6, 2048, 256, 256)

            def fBx():
                if want_k:
                    P.op("act", lambda e: e.copy(out=dst["va"][:, :, 0:64], in_=bank(3, 128, 128).rearrange("p (g d) -> p g d", g=2)),
                         reads=[RB[3]], writes=[dst["va_r"]])
                    if outs is not None:
                        P.op("act", lambda e: e.copy(out=ostage[:, 128:256], in_=bank(3, 128, 128)), reads=[RB[3]], writes=[R("ost_VA")])
                for (nm, src, so, n, gs, sso, br, go) in segs:
                    ng = n // gs
                    P.op("act", lambda e, src=src, so=so, n=n: e.activation(out=S1[:, so:so + n], in_=src, func=AF.Square), reads=[br], writes=[R("S1_" + nm)])
                    P.op("act", lambda e, src=src, so=so, n=n: e.copy(out=S2[:, so:so + n], in_=src), reads=[br], writes=[R("S2_" + nm)])
                    P.op("dve", lambda e, so=so, n=n, gs=gs, sso=sso, ng=ng: e.tensor_reduce(
                        out=ssq[:, sso:sso + ng], in_=S1[:, so:so + n].rearrange("p (g d) -> p g d", d=gs), axis=AX.X, op=ALU.add),
                        reads=[R("S1_" + nm)], writes=[R("ssq_" + nm)])
                    P.op("dve", lambda e, gs=gs, sso=sso, ng=ng: e.tensor_scalar(out=ssq[:, sso:sso + ng], in0=ssq[:, sso:sso + ng], scalar1=1.0 / gs, scalar2=EPS,
                                                                                 op0=ALU.mult, op1=ALU.add), reads=[R("ssq_" + nm)], writes=[R("ssq_" + nm)])
                if want_k:
                    P.op("dve", lambda e: e.tensor_copy(out=UT[:], in_=bank(3, 256, 256)), reads=[RB[3]], writes=[R("UT")])
                P.op("pool", lambda e: e.tensor_tensor(out=rq[:, 0:26], in0=ssq[:, 0:26], in1=cm05[:, 0:26], op=POWOP),
                     reads=[R("ssq_" + sg_[0]) for sg_ in segs] + [R("cm05")], writes=[R("rq")])

            def fBy():
                if want_q:
                    sg = SG[:, dst["sg"], :]
                    sgr = SGR[dst["sg"]]
                    for (b, c0, n, o) in ((4, 0, 512, 0), (6, 256, 256, 512), (7, 0, 256, 768)):
                        P.op("act", lambda e, b=b, c0=c0, n=n, o=o: e.activation(out=thb[:, o:o + n], in_=bank(b, c0, n), func=AF.Tanh, scale=0.5),
                             reads=[RB[b]], writes=[R("thb%d" % o)])
                        P.op("dve", lambda e, b=b, c0=c0, n=n, o=o: e.scalar_tensor_tensor(out=sg[:, o:o + n], in0=thb[:, o:o + n], scalar=1.0, in1=bank(b, c0, n),
                                                                                           op0=ALU.add, op1=ALU.mult),
                             reads=[R("thb%d" % o), RB[b]], writes=[R("sg%d_%d" % (dst["sg"], o))])
                if want_k:
                    P.op("act", lambda e: e.copy(out=dst["vb"][:, :, 0:64], in_=bank(6, 0, 256).rearrange("p (g d) -> p g d", g=4)),
                         reads=[RB[6]], writes=[dst["vb_r"]])
                    if outs is not None:
                        P.op("act", lambda e: e.copy(out=ostage[:, 512:768], in_=bank(6, 0, 256)), reads=[RB[6]], writes=[R("ost_VB")])

            def fB2a():
                if want_k:
                    for ch in range(2):
                        P.op("pe", lambda e, ch=ch: e.matmul(out=bank(0, ch * 256, 256), lhsT=UT[:, ch * 128:(ch + 1) * 128], rhs=bcs[:], start=True, stop=True),
                             reads=[R("UT"), R("bcs")], writes=[RB[0]])
                    P.op("act", lambda e: e.copy(out=ZW[:, dst["zw"], :], in_=bank(0)), reads=[RB[0]], writes=[ZWR[dst["zw"]]])
                for (nm, src, so, n, gs, sso, br, go) in segs:
                    ng = n // gs
                    P.op("dve", lambda e, so=so, n=n, gs=gs, sso=sso, ng=ng: e.tensor_tensor(
                        out=S2[:, so:so + n].rearrange("p (g d) -> p g d", d=gs), in0=S2[:, so:so + n].rearrange("p (g d) -> p g d", d=gs),
                        in1=rq[:, sso:sso + ng].unsqueeze(2).to_broadcast([128, ng, gs]), op=ALU.mult),
                        reads=[R("S2_" + nm), R("rq")], writes=[R("S2_" + nm)])

                def qdst_perm(nm, so, n, gs):
                    if nm == "QA":
                        return (qkb[:, 0:512].rearrange("p (j i d) -> p i j d", j=4, i=2, d=64), lambda ap: ap.rearrange("p (i j) d -> p i j d", i=2))
                    return (qkb[:, so:so + n].rearrange("p (g d) -> p g d", d=gs), lambda ap: ap)

                for (nm, src, so, n, gs, sso, br, go) in segs:
                    ng = n // gs
                    s2v = S2[:, so:so + n].rearrange("p (g d) -> p g d", d=gs)
                    gain = smallv[:, l, go:go + gs].unsqueeze(1).to_broadcast([128, ng, gs])
                    if ropev is None:
                        qdst, perm = qdst_perm(nm, so, n, gs)
                        gq = smallv[:, l, 0:64].unsqueeze(1).unsqueeze(1).to_broadcast([128, 2, 4, 64]) if nm == "QA" else gain
                        P.op("dve", lambda e, s2v=s2v, gq=gq, qdst=qdst, perm=perm: e.tensor_tensor(out=qdst, in0=perm(s2v), in1=gq, op=ALU.mult),
                             reads=[R("S2_" + nm), R("smallv")], writes=[R("qkb_" + nm)])
                        if outs is not None and nm in ("KA", "KB"):
                            oo = 0 if nm == "KA" else 256
                            P.op("dve", lambda e, s2v=s2v, gain=gain, oo=oo, n=n, gs=gs: e.tensor_tensor(
                                out=ostage[:, oo:oo + n].rearrange("p (g d) -> p g d", d=gs), in0=s2v, in1=gain, op=ALU.mult),
                                reads=[R("S2_" + nm), R("smallv")], writes=[R("ost_" + nm)])
                    else:
                        P.op("dve", lambda e, s2v=s2v, gain=gain: e.tensor_tensor(out=s2v, in0=s2v, in1=gain, op=ALU.mult),
                             reads=[R("S2_" + nm), R("smallv")], writes=[R("S2_" + nm)])
                if ropev is not None:
                    for ty, names, gs in (("A", ("QA", "KA"), 64), ("B", ("KB", "QB"), 32)):
                        sg_t = [sg_ for sg_ in segs if sg_[0] in names]
                        if not sg_t:
                            continue
                        lo = min(sg_[2] for sg_ in sg_t)
                        hi = max(sg_[2] + sg_[3] for sg_ in sg_t)
                        n = hi - lo
                        ng = n // gs
                        m = gs // 4
                        ro = 0 if gs == 64 else 128
                        cosv = ropev[:, ro:ro + gs].unsqueeze(1).to_broadcast([128, ng, gs])
                        sinv = ropev[:, ro + gs:ro + 2 * gs]
                        rd2 = [R("S2_" + sg_[0]) for sg_ in sg_t]
                        P.op("dve", lambda e, lo=lo, hi=hi, gs=gs, cosv=cosv: e.tensor_tensor(
                            out=S1[:, lo:hi].rearrange("p (g d) -> p g d", d=gs), in0=S2[:, lo:hi].rearrange("p (g d) -> p g d", d=gs), in1=cosv, op=ALU.mult),
                            reads=rd2 + [R("rope")], writes=[R("S1_" + sg_[0]) for sg_ in sg_t])
                        for xx in range(2):
                            src_ap = S2[:, lo:hi].rearrange("p (h r x m) -> p h r x m", r=2, x=2, m=m)[:, :, :, 1 - xx, :]
                            dst_ap = S4[:, lo:hi].rearrange("p (h r x m) -> p h r x m", r=2, x=2, m=m)[:, :, :, xx, :]
                            sn_ap = sinv.rearrange("p (r x m) -> p r x m", r=2, x=2, m=m)[:, :, xx, :].unsqueeze(1).to_broadcast([128, ng, 2, m])
                            P.op("pool", lambda e, src_ap=src_ap, dst_ap=dst_ap, sn_ap=sn_ap: e.tensor_tensor(out=dst_ap, in0=src_ap, in1=sn_ap, op=ALU.mult),
                                 reads=rd2 + [R("rope")], writes=[R("S4_%s%d" % (ty, xx))] + OST)
                        for (nm, src, so, n2, gs2, sso, br, go) in sg_t:
                            qdst, perm = qdst_perm(nm, so, n2, gs2)
                            s1v = S1[:, so:so + n2].rearrange("p (g d) -> p g d", d=gs2)
                            s4v = S4[:, so:so + n2].rearrange("p (g d) -> p g d", d=gs2)
                            P.op("pool", lambda e, s1v=s1v, s4v=s4v, qdst=qdst, perm=perm: e.tensor_tensor(out=qdst, in0=perm(s1v), in1=perm(s4v), op=ALU.add),
                                 reads=[R("S1_" + nm), R("S4_%s0" % ty), R("S4_%s1" % ty)] + OST, writes=[R("qkb_" + nm)])
                if outs is not None and want_k:
                    na_ap, nb_ap = outs
                    P.dma("sp", lambda e: e.dma_start(out=na_ap, in_=ostage[:, 0:256]), reads=[R("ost_KA"), R("ost_VA")], writes=[R("o_na")])
                    P.dma("sp", lambda e: e.dma_start(out=nb_ap, in_=ostage[:, 256:768]), reads=[R("ost_KB"), R("ost_VB")], writes=[R("o_nb")])

            def fB2b():
                tb = bank_bf(1)
                tb7 = bank_bf(7, 512, 512)
                if want_q:
                    for j in range(4):
                        P.op("pe", lambda e, j=j: e.transpose(out=tb[:, j * 128:(j + 1) * 128], in_=qkb[:, j * 128:(j + 1) * 128], identity=identb[:]),
                             reads=[R("qkb_QA"), R("identb")], writes=[RB[1]], sig=(j == 3))
                    P.op("dve", lambda e: e.tensor_copy(out=dst["qta"], in_=tb[:, 0:512].rearrange("p (j t) -> p j t", j=4)),
                         reads=[RB[1]], writes=[dst["qta_r"]])
                    for j in range(2):
                        P.op("pe", lambda e, j=j: e.transpose(out=tb7[:, j * 128:(j + 1) * 128], in_=qkb[:, 896 + j * 128:896 + (j + 1) * 128], identity=identb[:]),
                             reads=[R("qkb_QB"), R("identb")], writes=[RB[7]], sig=(j == 1))
                    for cc in range(2):
                        P.op("dve", lambda e, cc=cc: e.tensor_scalar(out=dst["qtb"][:, :, cc, :], in0=tb7[:, 0:256].rearrange("p (a t) -> p a t", a=2),
                                                                     scalar1=masks[:, cc:cc + 1], scalar2=None, op0=ALU.mult),
                             reads=[RB[7], R("masks")], writes=[dst["qtb_r"]])
                if want_k:
                    for j in range(3):
                        P.op("pe", lambda e, j=j: e.transpose(out=tb[:, 512 + j * 128:512 + (j + 1) * 128], in_=qkb[:, 512 + j * 128:512 + (j + 1) * 128], identity=identb[:]),
                             reads=[R("qkb_KA"), R("qkb_KB"), R("identb")], writes=[RB[1]], sig=(j == 2))
                    P.op("dve", lambda e: e.tensor_copy(out=dst["kta"], in_=tb[:, 512:640]), reads=[RB[1]], writes=[dst["kta_r"]])
                    P.op("dve", lambda e: e.tensor_copy(out=dst["ktb"], in_=tb[:, 640:896].rearrange("p (a t) -> p a t", a=2)), reads=[RB[1]], writes=[dst["ktb_r"]])

            return dict(F=fF, X=fX, Y=fY, Bx=fBx, By=fBy, B2a=fB2a, B2b=fB2b)

        att_step = [0]

        def attend(l, st_, qsl, nkc, sgslot, ft_sl, xt, xres, out_ap, out_res, alt=False):
            step = att_step

            def sbank():
                b = (step[0] % 2) * 2
                step[0] += 1
                return b

            def loop(inject=None):
                steps = []
                for kind in ("A", "B"):
                    for kc in range(nkc):
                        b0 = sbank()
                        buf = step[0] % 2
                        steps.append((kind, kc, b0, PT[:, buf, :], PTR[buf]))

                def qk(kind, kc, b0, pt, ptr):
                    ks = slice(kc * 128, (kc + 1) * 128)
                    if kind == "A":
                        for g in range(2):
                            P.op("pe", lambda e, g=g: e.matmul(out=bank(b0 + g), lhsT=st_["kta"][64 * g:64 * g + 64, ks],
                                                               rhs=st_["qta"][64 * g:64 * g + 64, :, qsl], start=True, stop=True),
                                 reads=[st_["kta_r"], st_["qta_r"]], writes=[RB[b0 + g]])
                    else:
                        for pr in range(2):
                            for i in range(2):
                                P.op("pe", lambda e, pr=pr, i=i: e.matmul(out=bank(b0 + i, pr * 256, 256), lhsT=st_["ktb"][64 * i:64 * i + 64, pr, ks],
                                                                          rhs=st_["qtb"][64 * i:64 * i + 64, pr, :, qsl], start=True, stop=True),
                                     reads=[st_["ktb_r"], st_["qtb_r"]], writes=[RB[b0 + i]], sig=True)

                def ex(kind, kc, b0, pt, ptr):
                    P.op("act", lambda e: e.activation(out=pt, in_=PS[:, b0 * 512:b0 * 512 + 1024], func=AF.Exp),
                         reads=[RB[b0], RB[b0 + 1]], writes=[ptr])

                def pv(kind, kc, b0, pt, ptr):
                    if kind == "A":
                        for g in range(2):
                            for j in range(4):
                                P.op("pe", lambda e, g=g, j=j: e.matmul(out=bank(4 + g, j * 65, 65), lhsT=pt[:, g * 512 + j * 128:g * 512 + (j + 1) * 128],
                                                                        rhs=st_["va"][:, kc, g, :], start=(kc == 0 and j == 0), stop=(kc == nkc - 1),
                                                                        skip_group_check=True),
                                     reads=[ptr, st_["va_r"]], writes=[RB[4 + g]], sig=(j == 3))
                    else:
                        for pr in range(2):
                            for i in range(2):
                                for cc in range(2):
                                    P.op("pe", lambda e, pr=pr, i=i, cc=cc: e.matmul(
                                        out=bank(6 + pr, (i * 2 + cc) * 65, 65), lhsT=pt[:, i * 512 + pr * 256 + cc * 128:i * 512 + pr * 256 + (cc + 1) * 128],
                                        rhs=st_["vb"][:, kc, 2 * pr + i, :], start=(kc == 0 and i == 0 and cc == 0), stop=(kc == nkc - 1), skip_group_check=True),
                                        reads=[ptr, st_["vb_r"]], writes=[RB[6 + pr]], sig=(i == 1 and cc == 1))

                qk(*steps[0])
                for si_, stp in enumerate(steps):
                    if si_ + 1 < len(steps):
                        qk(*steps[si_ + 1])
                    ex(*stp)
                    pv(*stp)
                    if inject and si_ in inject:
                        inject[si_]()

            sg = SG[:, sgslot, :]
            if alt:
                tA = SG[:, 2, :].bitcast(F32)
                tB = SG[:, 3, :].bitcast(F32)
                tA_r = [R("sg2_0"), R("sg2_512"), R("sg2_768")]
                tB_r = [R("sg3_0"), R("sg3_512"), R("sg3_768")]
            else:
                tA = S1[:, 0:512]
                tB = S2[:, 0:512]
                tA_r = [R("S1_QA")]
                tB_r = [R("S2_QA")]

            def post1():
                oav = PS[:, 4 * 512:6 * 512].rearrange("p (g x) -> p g x", g=2)[:, :, 0:260].rearrange("p g (j e) -> p g j e", e=65)
                P.op("dve", lambda e: e.reciprocal(out=recA[:].rearrange("p (g j) -> p g j", g=2), in_=oav[:, :, :, 64]), reads=[RB[4], RB[5]], writes=[R("recA")])
                for g in range(2):
                    P.op("dve", lambda e, g=g: e.tensor_tensor(out=tA[:, g * 256:(g + 1) * 256].rearrange("p (j d) -> p j d", j=4), in0=oav[:, g, :, 0:64],
                                                               in1=recA[:, g * 4:(g + 1) * 4].unsqueeze(2).to_broadcast([128, 4, 64]), op=ALU.mult),
                         reads=[RB[4 + g], R("recA")], writes=tA_r)
                obv = PS[:, 6 * 512:8 * 512].rearrange("p (g x) -> p g x", g=2)[:, :, 0:260].rearrange("p g (j e) -> p g j e", e=65)
                P.op("dve", lambda e: e.reciprocal(out=recB[:].rearrange("p (g j) -> p g j", g=2), in_=obv[:, :, :, 64]), reads=[RB[6], RB[7]], writes=[R("recB")])
                rbv = recB[:].rearrange("p (h c) -> p h c", c=2)
                P.op("dve", lambda e: e.tensor_scalar(out=rbv[:, :, 1], in0=rbv[:, :, 1], scalar1=lamn[:, l:l + 1], scalar2=None, op0=ALU.mult),
                     reads=[R("recB"), R("lamn")], writes=[R("recB")])
                for pr in range(2):
                    P.op("dve", lambda e, pr=pr: e.tensor_tensor(out=tB[:, pr * 256:(pr + 1) * 256].rearrange("p (j d) -> p j d", j=4), in0=obv[:, pr, :, 0:64],
                                                                 in1=recB[:, pr * 4:(pr + 1) * 4].unsqueeze(2).to_broadcast([128, 4, 64]), op=ALU.mult),
                         reads=[RB[6 + pr], R("recB")], writes=tB_r)

            def post2a(bo=2):
                P.op("dve", lambda e: e.tensor_tensor(out=mix[:, 0:512], in0=tA, in1=sg[:, 0:512], op=ALU.mult), reads=tA_r + [R("sg%d_0" % sgslot)], writes=[R("mix_A")])
                tBv = tB.rearrange("p (h c d) -> p h c d", h=4, c=2)
                obv3 = obt.rearrange("p (h d) -> p h d", h=4)
                P.op("dve", lambda e: e.tensor_tensor(out=obv3, in0=tBv[:, :, 0, :], in1=tBv[:, :, 1, :], op=ALU.add), reads=tB_r, writes=[R("OBT")])
                P.op("act", lambda e: e.activation(out=sqb, in_=obt, func=AF.Square), reads=[R("OBT")], writes=[R("SQB")])
                P.op("dve", lambda e: e.tensor_reduce(out=ssb[:, 0:4], in_=sqb.rearrange("p (h d) -> p h d", h=4), axis=AX.X, op=ALU.add),
                     reads=[R("SQB")], writes=[R("ssb")])
                P.op("dve", lambda e: e.tensor_scalar(out=ssb[:, 0:4], in0=ssb[:, 0:4], scalar1=1.0 / 64, scalar2=EPS, op0=ALU.mult, op1=ALU.add),
                     reads=[R("ssb")], writes=[R("ssb")])
                P.op("pool", lambda e: e.tensor_tensor(out=ssb[:, 4:8], in0=ssb[:, 0:4], in1=cm05[:, 0:4], op=POWOP), reads=[R("ssb"), R("cm05")], writes=[R("ssb")])
                P.op("dve", lambda e: e.tensor_tensor(out=obv3, in0=obv3, in1=ssb[:, 4:8].unsqueeze(2).to_broadcast([128, 4, 64]), op=ALU.mult),
                     reads=[R("OBT"), R("ssb")], writes=[R("OBT")])
                P.op("dve", lambda e: e.tensor_tensor(out=obv3, in0=obv3, in1=smallv[:, l, 192:256].unsqueeze(1).to_broadcast([128, 4, 64]), op=ALU.mult),
                     reads=[R("OBT"), R("smallv")], writes=[R("OBT")])
                P.op("dve", lambda e: e.tensor_tensor(out=mix[:, 512:768], in0=obt, in1=sg[:, 512:768], op=ALU.mult), reads=[R("OBT"), R("sg%d_512" % sgslot)], writes=[R("mix_B")])
                for c in range(2):
                    P.op("pe", lambda e, c=c: e.matmul(out=bank(bo, 0, 256), lhsT=FT[:, c, ft_sl], rhs=wc[:, l, c, :], start=(c == 0), stop=(c == 1)),
                         reads=[R("FT"), R("wc")], writes=[RB[bo]], sig=(c == 1))
                P.op("dve", lambda e: e.tensor_tensor(out=mix[:, 768:1024], in0=bank(bo, 0, 256), in1=sg[:, 768:1024], op=ALU.mult),
                     reads=[RB[bo], R("sg%d_768" % sgslot)], writes=[R("mix_C")])

            def post2b(bt=3):
                mtb = bank_bf(bt)
                for c in range(8):
                    P.op("pe", lambda e, c=c: e.transpose(out=mtb[:, c * 128:(c + 1) * 128], in_=mix[:, c * 128:(c + 1) * 128], identity=identb[:]),
                         reads=[R("mix_A"), R("mix_B"), R("mix_C"), R("identb")], writes=[RB[bt]], sig=(c == 7))
                P.op("dve", lambda e: e.tensor_copy(out=mixT[:].rearrange("p c t -> p (c t)"), in_=mtb), reads=[RB[bt]], writes=[R("mixT")])

            def post2c(by=0):
                for hf in range(2):
                    for k in range(8):
                        P.op("pe", lambda e, hf=hf, k=k: e.matmul(out=bank(by + hf), lhsT=mixT[:, k, :], rhs=wout[:, k, hf * 512:(hf + 1) * 512], start=(k == 0), stop=(k == 7)),
                             reads=[R("mixT"), WOUT[k]], writes=[RB[by + hf]], sig=(k == 7))
                yt = S1[:, 0:1024]
                ytr = [R("S1_QA"), R("S1_KA"), R("S1_KB"), R("S1_QB")]
                P.op("dve", lambda e: e.tensor_tensor(out=yt, in0=PS[:, by * 512:by * 512 + 1024], in1=gb[:], op=ALU.mult), reads=[RB[by], RB[by + 1], R("gb")], writes=ytr)
                P.op("pool", lambda e: e.tensor_tensor(out=xt, in0=xt, in1=yt, op=ALU.add), reads=ytr + [xres], writes=[xres])
                if out_ap is not None:
                    P.dma("sp", lambda e: e.dma_start(out=out_ap, in_=xt), reads=[xres], writes=[out_res])

            def post2():
                post2a(2)
                post2b(3)
                post2c(0)

            return (loop, post1, post2, post2a, post2b, post2c)

        def run_tiles(items, inj=None, first_inject=None, defer_last=False):
            n = len(items)
            if first_inject is not None:
                items[0][0](first_inject)
            else:
                items[0][0]()
            items[0][1]()
            for i in range(1, n):
                if inj is None:
                    items[i][0]()
                    items[i - 1][2]()
                else:
                    pa, pb, pc = items[i - 1][3], items[i - 1][4], items[i - 1][5]
                    items[i][0]({inj[0]: (lambda pa=pa: pa(6)), inj[1]: (lambda pb=pb: pb(7)), inj[2]: (lambda pc=pc: pc(6))})
                items[i][1]()
            if defer_last:
                return items[n - 1][3:6]
            items[n - 1][2]()
            return None

        def fourier_prompt():
            for c in range(2):
                for s2 in range(2):
                    for zw in range(2):
                        P.op("pe", lambda e, c=c, s2=s2, zw=zw: e.matmul(out=bank(2, c * 256, 256), lhsT=ZW[:, s2, c * 256 + zw * 128:c * 256 + (zw + 1) * 128],
                                                                         rhs=dftp[:, s2, zw, :], start=(s2 == 0 and zw == 0), stop=(s2 == 1 and zw == 1)),
                             reads=[ZWR[s2], R("dftp")], writes=[RB[2]], sig=(s2 == 1 and zw == 1))
            P.op("act", lambda e: e.copy(out=FT[:, :, 0:256], in_=bank(2).rearrange("p (c s) -> p c s", c=2)), reads=[RB[2]], writes=[R("FT")])

        def fourier_inject(blk):
            def sdma(s2):
                slot = s2 % 4
                rv = ring[:, slot, :].rearrange("p (z s) -> p z s", z=2)
                P.dma("sp", lambda e: e.dma_start(out=rv, in_=d_dfts[s2].rearrange("p (z s) -> p z s", z=2)[:, :, blk * 512:(blk + 1) * 512]), writes=[RING[slot]])

            def stile(s2):
                slot = s2 % 4
                rv = ring[:, slot, :].rearrange("p (z s) -> p z s", z=2)
                for c in range(2):
                    for zw in range(2):
                        P.op("pe", lambda e, c=c, zw=zw: e.matmul(out=bank(6 + c), lhsT=ZW[:, s2, c * 256 + zw * 128:c * 256 + (zw + 1) * 128],
                                                                  rhs=rv[:, zw, :], start=(s2 == 0 and zw == 0),
                                                                  stop=(s2 == 7 and zw == 1)),
                             reads=[ZWR[s2], RING[slot]], writes=[RB[6 + c]], sig=(zw == 1))
                if s2 + 4 < 8:
                    sdma(s2 + 4)

            for s2_ in range(4):
                sdma(s2_)

            def evac():
                for c in range(2):
                    P.op("dve", lambda e, c=c: e.tensor_copy(out=FT[:, c, :], in_=bank(6 + c)), reads=[RB[6 + c]], writes=[R("FT")])
            d = {i: (lambda i=i: stile(i)) for i in range(8)}
            d[8] = evac
            return d

        def run_p1(tiles, after_a=None, final_hook=None, pre_hook=None, mid_hook=None):
            T = tiles
            n = len(T)
            hook = after_a if after_a else (lambda i: None)
            T[0]["F"]()
            T[0]["X"]()
            if pre_hook:
                pre_hook()
            if n > 1:
                T[1]["F"]()
            T[0]["Y"]()
            T[0]["Bx"]()
            hook(0)
            for t in range(1, n):
                T[t]["X"]()
                T[t - 1]["By"]()
                T[t - 1]["B2a"]()
                if mid_hook:
                    mid_hook(t - 1)
                if t + 1 < n:
                    T[t + 1]["F"]()
                T[t]["Y"]()
                T[t]["Bx"]()
                T[t - 1]["B2b"]()
                hook(t)
            T[n - 1]["By"]()
            T[n - 1]["B2a"]()
            if final_hook:
                final_hook()
            T[n - 1]["B2b"]()

        sstore = dict(qta=sQTA, qtb=sQTB, kta=sKTA, ktb=sKTB, va=sVA, vb=sVB,
                      qta_r=ARENA_S[0], qtb_r=ARENA_S[1], kta_r=ARENA_S[2], ktb_r=ARENA_S[3], va_r=ARENA_S[4], vb_r=ARENA_S[5])
        pstore = dict(qta=pQTA, qtb=pQTB, kta=pKTA, ktb=pKTB, va=pVA, vb=pVB,
                      qta_r=R("pQTA"), qtb_r=R("pQTB"), kta_r=R("pKTA"), ktb_r=R("pKTB"), va_r=R("pVA"), vb_r=R("pVB"))

        def sample_dst(t, qslot):
            d = dict(sg=qslot, zw=t, va=sVA[:, t, :, :], vb=sVB[:, t, :, :], va_r=ARENA_S[4], vb_r=ARENA_S[5],
                     kta=sKTA[:, t * 128:(t + 1) * 128], ktb=sKTB[:, :, t * 128:(t + 1) * 128], kta_r=ARENA_S[2], ktb_r=ARENA_S[3],
                     qta_r=ARENA_S[0], qtb_r=ARENA_S[1])
            if qslot is not None:
                d["qta"] = sQTA[:, :, qslot * 128:(qslot + 1) * 128]
                d["qtb"] = sQTB[:, :, :, qslot * 128:(qslot + 1) * 128]
            return d

        def sample_layer(l, nblk):
            P.op("pool", lambda e: e.memset(sVA, 1.0), reads=XP, writes=ARENA_S + XP + ZWR)
            P.op("pool", lambda e: e.memset(sVB, 1.0), writes=[ARENA_S[5]])
            for g in range(2):
                P.dma("pool", lambda e, g=g: e.dma_start(out=sVA[:, 8:12, g, 0:64], in_=d_ca[l][:, 128 + g * 64:192 + g * 64].rearrange("(c p) d -> p c d", p=128)),
                      reads=[ARENA_S[4]], writes=[R("sVAc%d" % g)])
            for g in range(4):
                P.dma("pool", lambda e, g=g: e.dma_start(out=sVB[:, 8:12, g, 0:64], in_=d_cb[l][:, 256 + g * 64:320 + g * 64].rearrange("(c p) d -> p c d", p=128)),
                      reads=[ARENA_S[5]], writes=[R("sVBc%d" % g)])
            P.dma("pool", lambda e: e.dma_start(out=sCK[:, :, 0:128], in_=d_ca[l][:, 0:128].rearrange("(c p) d -> p c d", p=128)), reads=[ARENA_S[6]], writes=[R("ckA")])
            P.dma("pool", lambda e: e.dma_start(out=sCK[:, :, 128:384], in_=d_cb[l][:, 0:256].rearrange("(c p) d -> p c d", p=128)), reads=[ARENA_S[6]], writes=[R("ckB")])
            P.op("pool", lambda e: e.memset(ssb[:, 1:2], 0.0), reads=[R("sVAc%d" % c_) for c_ in range(2)] + [R("sVBc%d" % c_) for c_ in range(4)],
                 writes=[ARENA_S[4], ARENA_S[5], R("ssb")])

            def cache_chunk(c):
                tb = bank_bf(1)
                for j in range(3):
                    P.op("pe", lambda e, j=j: e.transpose(out=tb[:, j * 128:(j + 1) * 128], in_=sCK[:, c, j * 128:(j + 1) * 128], identity=identb[:]),
                         reads=[R("ckA"), R("ckB"), R("identb")], writes=[RB[1]], sig=(j == 2))
                P.op("dve", lambda e: e.tensor_copy(out=sKTA[:, 1024 + c * 128:1024 + (c + 1) * 128], in_=tb[:, 0:128]), reads=[RB[1]], writes=[ARENA_S[2]])
                P.op("dve", lambda e: e.tensor_copy(out=sKTB[:, :, 1024 + c * 128:1024 + (c + 1) * 128], in_=tb[:, 128:384].rearrange("p (a t) -> p a t", a=2)),
                     reads=[RB[1]], writes=[ARENA_S[3]])
                if c == 3:
                    P.op("pool", lambda e: e.memset(ssb[:, 3:4], 0.0), reads=[R("ckA"), R("ckB")], writes=[ARENA_S[6], R("ssb")])
            P.mark("s_cache")
            sc0 = 0 if l == 0 else 16
            if l == 0:
                xstats([xs[:, t, :] for t in range(8)], XS, 0)
            P.mark("s_xstats")
            tiles = []
            for t in range(8):
                half, tt = t // 4, t % 4
                d_ = phase1(l, 1, xs[:, t, :], XS[t], sc0 + t, "full" if half == 0 else "kvu", sample_dst(t, tt if half == 0 else None), ropev=rope[:, tt, :], par=t % 2)
                if tt == 0:
                    def b2a_(f=d_["B2a"], half=half):
                        P.dma("sp", lambda e: e.dma_start(out=rope[:], in_=d_rope[half * 4:(half + 1) * 4].rearrange("t p n -> p t n")), writes=[R("rope")])
                        f()
                    d_["B2a"] = b2a_
                tiles.append(d_)
            def after_a(i):
                if i < 4:
                    cache_chunk(i)
                if l == 0 and i == 2:
                    load_wout(0)
                if l == 0 and i < 4:
                    mod_stream([(0, 16 + 2 * i), (0, 17 + 2 * i)])
                if l == 0 and i == 4:
                    mod_flush()
                if i == 5:
                    build_gate(l, 1)
            run_p1(tiles, after_a)
            P.mark("s_p1")
            for blk in range(nblk):
                if blk == 1:
                    tiles = []
                    for tt in range(4):
                        t = 4 + tt
                        tiles.append(phase1(l, 1, xs[:, t, :], XS[t], sc0 + t, "qg", sample_dst(t, tt), ropev=rope[:, tt, :], par=tt % 2))
                    dpa, dpb, dpc = deferred

                    def hk(i):
                        if i == 0:
                            dpb(1)
                            dpc(0)
                    run_p1(tiles, after_a=hk, pre_hook=lambda: dpa(0))
                items = []
                for tt in range(4):
                    t = blk * 4 + tt
                    last = (l == 1)
                    items.append(attend(l, sstore, slice(tt * 128, (tt + 1) * 128), 12, tt, slice(tt * 128, (tt + 1) * 128), xs[:, t, :], XS[t],
                                        o_ys[t * 128:(t + 1) * 128, :] if last else None, R("o_ys")))
                deferred = run_tiles(items, inj=(0, 6, 7), first_inject=fourier_inject(blk), defer_last=(nblk == 2 and blk == 0))
                P.mark("s_att_%d" % blk)

        def prompt_layer(l, after_p1=None, carry=None, keep_pending=False):
            if carry is None:
                build_gate(l, 0)
            P.op("pool", lambda e: e.memset(ssb[:, 0:1], 0.0), writes=ZWR + [R("ssb")])
            if l == 0:
                for t in range(8):
                    P.dma("sp", lambda e, t=t: e.dma_start(out=xp[:, t, :], in_=d_xp[t * 128:(t + 1) * 128, :]), reads=[], writes=[XP[t]] + ARENA_S)
                P.op("pool", lambda e: e.memset(pVA[:], 1.0), writes=[pstore["va_r"]])
                P.op("pool", lambda e: e.memset(pVB[:], 1.0), writes=[pstore["vb_r"]])
            xstats([xp[:, t, :] for t in range(2)], XP[0:2], 8)
            pend = {}
            if carry is not None:
                pend.update(carry)
                pend[1] = list(pend.get(1, [])) + [lambda: build_gate(l, 0), lambda: load_wout(l)]
            for s in range(4):
                tiles = []
                for ti in range(2):
                    t = s * 2 + ti
                    d = dict(sg=ti, zw=ti, va=pVA[:, ti, :, :], vb=pVB[:, ti, :, :], va_r=pstore["va_r"], vb_r=pstore["vb_r"],
                             kta=pKTA[:, ti * 128:(ti + 1) * 128], ktb=pKTB[:, :, ti * 128:(ti + 1) * 128], kta_r=ZWR[6], ktb_r=ZWR[7],
                             qta=pQTA[:, :, ti * 128:(ti + 1) * 128], qtb=pQTB[:, :, :, ti * 128:(ti + 1) * 128], qta_r=ZWR[2], qtb_r=ZWR[4])
                    tiles.append(phase1(l, 0, xp[:, t, :], XP[t], 8 + t, "full", d, ropev=None,
                                        outs=(o_na[s, l, ti * 128:(ti + 1) * 128, :], o_nb[s, l, ti * 128:(ti + 1) * 128, :]), par=ti))
                def after_a(i, s=s):
                    if i == 0 and s < 3:
                        xstats([xp[:, t, :] for t in range(2 * s + 2, 2 * s + 4)], XP[2 * s + 2:2 * s + 4], 8 + 2 * s + 2)
                    if l == 0:
                        tix = s * 2 + i
                        mod_stream([(1, bb) for bb in range(tix * 3, tix * 3 + 3)])
                        if tix == 7:
                            mod_flush()
                    if i == 0:
                        for f in pend.pop(0, []):
                            f()

                def final_hook():
                    for f in pend.pop(1, []):
                        f()
                def pre_hook():
                    for f in pend.pop("pre", []):
                        f()
                def mid_hook(i):
                    for f in pend.pop("mid%d" % i, []):
                        f()
                run_p1(tiles, after_a, final_hook, pre_hook, mid_hook)
                if s == 3 and after_p1 is not None:
                    after_p1()
                fourier_prompt()
                pst = dict(pstore)
                pst.update(qta_r=ZWR[2], qtb_r=ZWR[4], kta_r=ZWR[6], ktb_r=ZWR[7])
                items = []
                for ti in range(2):
                    t = s * 2 + ti
                    items.append(attend(l, pst, slice(ti * 128, (ti + 1) * 128), 2, ti, slice(ti * 128, (ti + 1) * 128), xp[:, t, :], XP[t],
                                        o_yp[t * 128:(t + 1) * 128, :] if l == 1 else None, R("o_yp"), alt=(ti == 1)))
                items[0][0]()
                items[0][1]()
                items[1][0]()
                items[1][1]()
                i0_, i1_ = items
                pend["pre"] = [lambda f=i0_[3]: f(0)]
                pend[0] = [lambda f=i0_[4]: f(1), lambda f=i0_[5]: f(0)]
                pend["mid0"] = [lambda f=i1_[3]: f(0)]
                pend[1] = [lambda f=i1_[4]: f(1), lambda f=i1_[5]: f(0)]
            if keep_pending:
                return pend
            for i in ("pre", 0, "mid0", 1):
                for f in pend.pop(i, []):
                    f()
            return None

        P.mark("setup")
        mod_super(0, 0)
        load_win(0, parts=tuple(range(8)))
        mod_super(0, 1)
        load_win(0, parts=tuple(range(8, 16)))
        P.mark("mod0")
        P.mark("weights0")
        sample_layer(0, 2)
        xstats([xs[:, t, :] for t in range(8)], XS, 16)
        P.mark("sample0")
        carry = prompt_layer(0, after_p1=lambda: load_win(1), keep_pending=True)
        prompt_layer(1, carry=carry)
        sample_layer(1, 1)
        P.finish([R("o_ys"), R("o_yp"), R("o_na"), R("o_nb")])
        P.build(nc, st)
        print("n_inst", P.n_inst, {n: len(e.prog) for n, e in P.e.items()})
    return nc


def _bf16(a):
    return np.asarray(a, dtype=np.float32).astype(ml_dtypes.bfloat16)


def _consts():
    identf = np.eye(128, dtype=np.float32)
    ch = np.arange(64)
    ang = 2 * np.pi * np.outer(ch, ch) / 64.0
    c64, s64 = np.cos(ang) / 8.0, np.sin(ang) / 8.0
    bcs = np.zeros((128, 256), np.float64)
    for g in range(2):
        bcs[g * 64:(g + 1) * 64, g * 64:(g + 1) * 64] = c64
        bcs[g * 64:(g + 1) * 64, 128 + g * 64:128 + (g + 1) * 64] = s64
    sp = np.arange(256)
    a = 2 * np.pi * np.outer(sp, sp) / 256.0
    cp, sn = np.cos(a) / 16.0, -np.sin(a) / 16.0
    dftp = np.stack([np.stack([cp[t * 128:(t + 1) * 128], sn[t * 128:(t + 1) * 128]], 1) for t in range(2)], 1)
    masks = np.zeros((128, 2), np.float32)
    masks[:, 0] = ((np.arange(128) % 64) < 32)
    masks[:, 1] = 1.0 - masks[:, 0]
    return identf, _bf16(identf), _bf16(bcs), _bf16(dftp.reshape(128, 1024)), masks


def _sample_tables(par):
    pos = np.concatenate([np.arange(512) + 512 * par, np.arange(512) + 512 * (1 - par)])
    a = 2 * np.pi * (np.outer(pos, pos) % 1024) / 1024.0
    c, s = np.cos(a) / 32.0, -np.sin(a) / 32.0
    dfts = np.stack([c, s], 1).reshape(8, 128, 2048)
    row = (pos // 64).astype(np.float64)
    col = (pos % 64).astype(np.float64)
    tabs = []
    for dim in (64, 32):
        nf = dim // 4
        inv = 1.0 / (10000.0 ** (np.arange(nf, dtype=np.float64) / nf))
        inv = inv.astype(np.float32).astype(np.float64)
        ar = (row[:, None].astype(np.float32) * inv.astype(np.float32)).astype(np.float64)
        ac = (col[:, None].astype(np.float32) * inv.astype(np.float32)).astype(np.float64)
        cosv = np.concatenate([np.cos(ar), np.cos(ar), np.cos(ac), np.cos(ac)], 1)
        sinv = np.concatenate([-np.sin(ar), np.sin(ar), -np.sin(ac), np.sin(ac)], 1)
        tabs += [cosv, sinv]
    rope = np.concatenate(tabs, 1).astype(np.float32).reshape(8, 128, 192)
    return _bf16(dfts), rope


_NC_CACHE = {}


def _prep(x_prompt, x_sample, cache_attn_a, cache_attn_b, c, c_ctx, norm_g, w_mod, b_mod, w_in,
          q_norm_a, k_norm_a, q_norm_b, k_norm_b, lambda_q1, lambda_k1, lambda_q2, lambda_k2,
          subln_g, w_fourier, w_out):
    f = lambda a: np.ascontiguousarray(np.asarray(a, dtype=np.float32))
    x_prompt, x_sample, cache_attn_a, cache_attn_b = f(x_prompt), f(x_sample), f(cache_attn_a), f(cache_attn_b)
    c, c_ctx, norm_g, w_mod, b_mod, w_in = f(c), f(c_ctx), f(norm_g), f(w_mod), f(b_mod), f(w_in)
    w_fourier, w_out = f(w_fourier), f(w_out)
    identf, identb, bcs, dftp, masks = _consts()
    tabs = [_sample_tables(0), _sample_tables(1)]
    smallv = np.concatenate([f(q_norm_a), f(k_norm_a), f(k_norm_b), f(q_norm_b), f(subln_g),
                             f(lambda_q1), f(lambda_k1), f(lambda_q2), f(lambda_k2)], axis=1)
    bmodT = np.ascontiguousarray(b_mod.reshape(2, 24, 128).transpose(2, 0, 1).reshape(128, 48))
    normgT = np.ascontiguousarray(norm_g.reshape(2, 8, 128).transpose(2, 0, 1).reshape(128, 16))
    in_maps = []
    for i in range(8):
        b, par = i // 2, i % 2
        xs = np.concatenate([x_sample[b, 512 * par:512 * par + 512], x_sample[b, 512 * (1 - par):512 * (1 - par) + 512]], 0)
        condT = np.stack([c_ctx.reshape(8, 128).T, c[b].reshape(8, 128).T], axis=2).reshape(128, 16)
        in_maps.append(dict(
            xp=np.ascontiguousarray(x_prompt[4 * i:4 * i + 4].reshape(1024, 1024)), xs=np.ascontiguousarray(xs),
            ca=np.ascontiguousarray(cache_attn_a[b].reshape(2, 512, 256)), cb=np.ascontiguousarray(cache_attn_b[b].reshape(2, 512, 512)),
            condT=np.ascontiguousarray(condT), bmodT=bmodT, normgT=normgT, smallv=smallv,
            w_mod=w_mod, w_in=w_in, w_out=w_out, w_fourier=w_fourier,
            identf=identf, identb=identb, bcs=bcs, dftp=dftp, dfts=tabs[par][0], rope=tabs[par][1], masks=masks))
    return in_maps


def kernel(**inputs):
    in_maps = _prep(**inputs)
    if "nc" not in _NC_CACHE:
        _NC_CACHE["nc"] = build_program()
    nc = _NC_CACHE["nc"]
    res = run_bass_kernel_spmd(nc, in_maps, core_ids=list(range(8)))
    y_prompt = np.zeros((32, 256, 1024), np.float32)
    y_sample = np.zeros((4, 1024, 1024), np.float32)
    new_a = np.zeros((32, 2, 256, 2, 2, 64), np.float32)
    new_b = np.zeros((32, 2, 256, 2, 4, 64), np.float32)
    for i in range(8):
        r = res.results[i]
        b, par = i // 2, i % 2
        y_prompt[4 * i:4 * i + 4] = r["yp"].reshape(4, 256, 1024)
        y_sample[b, 512 * par:512 * par + 512] = r["ys"]
        new_a[4 * i:4 * i + 4] = r["na"].reshape(4, 2, 256, 2, 2, 64)
        new_b[4 * i:4 * i + 4] = r["nb"].reshape(4, 2, 256, 2, 4, 64)
    return (y_prompt, y_sample, new_a, new_b)
```
